# Optimizing a Trainium2 kernel written in Bass

```python
import math
import jax, jax.numpy as jnp
from jax import lax
import numpy as np

D_MODEL = 1024
BATCH = 8
SEQ = 2048
DEPTH = 1

D_A = D_MODEL
SGU_CHUNK = 128
SGU_GROUPS = 8
RWKV_HEAD = 64
D_B = D_MODEL
N_HEADS_B = D_B // RWKV_HEAD

def _lora_dim(factor, power):
    return max(32, int(round(factor * D_MODEL ** power / 32)) * 32)

LORA_W = _lora_dim(1.8, 0.5)
LORA_A = _lora_dim(1.8, 0.5)
LORA_G = _lora_dim(0.6, 0.8)
C_B = 3 * D_B + LORA_W + LORA_A + LORA_G
P_TOTAL = 2 * D_A + C_B + 2 * D_MODEL
D_FF = 4 * D_MODEL
NORM_EPS = 1e-6
LN_EPS = 1e-5
GN_EPS = 64e-5

kernel_name = "hybrid_gmlp_rwkv7_gated_block"


def _rms_norm(x, g):
    xf = x.astype(jnp.float32)
    y = xf * lax.rsqrt(jnp.mean(xf * xf, axis=-1, keepdims=True) + NORM_EPS)
    return (y * g.astype(jnp.float32)).astype(x.dtype)


def _layer_norm(x, g, b):
    xf = x.astype(jnp.float32)
    mu = jnp.mean(xf, axis=-1, keepdims=True)
    var = jnp.mean(jnp.square(xf - mu), axis=-1, keepdims=True)
    y = (xf - mu) * lax.rsqrt(var + LN_EPS)
    return (y * g.astype(jnp.float32) + b.astype(jnp.float32)).astype(x.dtype)


def _token_shift(p):
    return jnp.pad(p, ((0, 0), (1, 0), (0, 0)))[:, :-1, :]


def _sgu_branch(p_sgu, ln_w, ln_b, sgu_w, sgu_b, w_proj_a):
    B, T, _ = p_sgu.shape
    z = jax.nn.gelu(p_sgu, approximate=False)
    u, v = jnp.split(z, 2, axis=-1)
    v = _layer_norm(v, ln_w, ln_b)
    n_chunks = T // SGU_CHUNK
    dg = D_A // SGU_GROUPS
    v = v.reshape(B, n_chunks, SGU_CHUNK, SGU_GROUPS, dg)
    mask = jnp.tril(jnp.ones((SGU_CHUNK, SGU_CHUNK), dtype=sgu_w.dtype))
    ws = sgu_w * mask[None]
    sv = jnp.einsum('gij,bcjgd->bcigd', ws, v)
    sv = sv + jnp.swapaxes(sgu_b, 0, 1)[None, None, :, :, None]
    s = u * sv.reshape(B, T, D_A)
    return s @ w_proj_a


def _rwkv7_scan(r, w, k, v, kk, a):
    B, T, H, N = r.shape
    xs = tuple(jnp.moveaxis(t, 1, 0) for t in (r, w, k, v, kk, a))

    def step(S, inp):
        r_t, w_t, k_t, v_t, kk_t, a_t = inp
        sa = jnp.einsum('bhvk,bhk->bhv', S, -kk_t)
        S = (S * w_t[:, :, None, :]
             + sa[..., None] * (kk_t * a_t)[:, :, None, :]
             + v_t[..., None] * k_t[:, :, None, :])
        o = jnp.einsum('bhvk,bhk->bhv', S, r_t)
        return S, o

    S0 = jnp.zeros((B, H, N, N), dtype=jnp.float32)
    _, o = lax.scan(step, S0, xs)
    return jnp.moveaxis(o, 0, 1)


def _rwkv7_branch(p_rwkv, shift_b, w_lora_w, w0, a_lora_w, a0, g_lora_w,
                  k_k, k_a, r_k, ln_x_w, ln_x_b, w_proj_b):
    B, T, _ = p_rwkv.shape
    f32 = jnp.float32
    p = p_rwkv.astype(f32)
    sb = shift_b.astype(f32)
    q = p * sb[0] + _token_shift(p) * sb[1]
    cuts = np.cumsum([D_B, D_B, D_B, LORA_W, LORA_A]).tolist()
    r, k, v, xw, xa, xg = jnp.split(q, cuts, axis=-1)
    w = -jax.nn.softplus(-(w0.astype(f32) + jnp.tanh(xw) @ w_lora_w.astype(f32))) - 0.5
    decay = jnp.exp(-jnp.exp(w))
    aa = jax.nn.sigmoid(a0.astype(f32) + xa @ a_lora_w.astype(f32))
    g = jax.nn.sigmoid(xg) @ g_lora_w.astype(f32)
    kk = (k * k_k.astype(f32)).reshape(B, T, N_HEADS_B, RWKV_HEAD)
    kk = kk / jnp.maximum(jnp.linalg.norm(kk, axis=-1, keepdims=True), 1e-12)
    k = k * (1.0 + (aa - 1.0) * k_a.astype(f32))
    hs = lambda t: t.reshape(B, T, N_HEADS_B, RWKV_HEAD)
    rh, kh, vh = hs(r), hs(k), hs(v)
    o = _rwkv7_scan(rh, hs(decay), kh, vh, kk, hs(aa))
    mu = jnp.mean(o, axis=-1, keepdims=True)
    var = jnp.mean(jnp.square(o - mu), axis=-1, keepdims=True)
    o = ((o - mu) * lax.rsqrt(var + GN_EPS)).reshape(B, T, D_B)
    o = o * ln_x_w.astype(f32) + ln_x_b.astype(f32)
    r_k_h = r_k.astype(f32).reshape(N_HEADS_B, RWKV_HEAD)
    bonus = jnp.sum(rh * kh * r_k_h, axis=-1, keepdims=True) * vh
    o = (o + bonus.reshape(B, T, D_B)) * g
    return (o @ w_proj_b.astype(f32)).astype(p_rwkv.dtype)


def setup_inputs(seed: int = 0) -> dict:
    key = jax.random.key(seed)
    ks = jax.random.split(key, 32)
    L = DEPTH
    f32 = jnp.float32

    def nrm(k, shape, scale):
        return jax.random.normal(k, shape, f32) * scale

    mu = jax.random.uniform(ks[8], (L, C_B), f32)
    return {
        "x": nrm(ks[0], (BATCH, SEQ, D_MODEL), 1.0),
        "g_mix": 1.0 + nrm(ks[1], (L, D_MODEL), 0.02),
        "w_in": nrm(ks[2], (L, D_MODEL, P_TOTAL), D_MODEL ** -0.5),
        "sgu_ln_w": 1.0 + nrm(ks[3], (L, D_A), 0.02),
        "sgu_ln_b": nrm(ks[4], (L, D_A), 0.02),
        "sgu_w": nrm(ks[5], (L, SGU_GROUPS, SGU_CHUNK, SGU_CHUNK), SGU_CHUNK ** -0.5),
        "sgu_b": 1.0 + nrm(ks[6], (L, SGU_GROUPS, SGU_CHUNK), 0.02),
        "w_proj_a": nrm(ks[7], (L, D_A, D_MODEL), D_A ** -0.5),
        "shift_b": jnp.stack([1.0 - mu, mu], axis=1),
        "w_lora_w": nrm(ks[9], (L, LORA_W, D_B), 0.1 * LORA_W ** -0.5),
        "w0": jax.random.uniform(ks[10], (L, D_B), f32, minval=-4.0, maxval=1.0),
        "a_lora_w": nrm(ks[11], (L, LORA_A, D_B), LORA_A ** -0.5),
        "a0": nrm(ks[12], (L, D_B), 0.1),
        "g_lora_w": nrm(ks[13], (L, LORA_G, D_B), LORA_G ** -0.5),
        "k_k": 0.85 + nrm(ks[14], (L, D_B), 0.02),
        "k_a": 1.0 + nrm(ks[15], (L, D_B), 0.02),
        "r_k": nrm(ks[16], (L, D_B), 0.1),
        "ln_x_w": 1.0 + nrm(ks[17], (L, D_B), 0.02),
        "ln_x_b": nrm(ks[18], (L, D_B), 0.02),
        "w_proj_b": nrm(ks[19], (L, D_B, D_MODEL), D_B ** -0.5),
        "w_out": nrm(ks[20], (L, D_MODEL, D_MODEL), D_MODEL ** -0.5),
        "g_ffn": 1.0 + nrm(ks[21], (L, D_MODEL), 0.02),
        "w_ffn1": nrm(ks[22], (L, D_MODEL, D_FF), D_MODEL ** -0.5),
        "w_ffn2": nrm(ks[23], (L, D_FF, D_MODEL), D_FF ** -0.5),
        "g_final": 1.0 + nrm(ks[24], (D_MODEL,), 0.02),
    }


def reference(x, g_mix, w_in, sgu_ln_w, sgu_ln_b, sgu_w, sgu_b, w_proj_a, shift_b,
              w_lora_w, w0, a_lora_w, a0, g_lora_w, k_k, k_a, r_k, ln_x_w, ln_x_b,
              w_proj_b, w_out, g_ffn, w_ffn1, w_ffn2, g_final):
    h = x
    for l in range(DEPTH):
        a = _rms_norm(h, g_mix[l])
        p = a @ w_in[l]
        p_sgu, p_rwkv, p_gate = jnp.split(p, [2 * D_A, 2 * D_A + C_B], axis=-1)
        y_a = _sgu_branch(p_sgu, sgu_ln_w[l], sgu_ln_b[l], sgu_w[l], sgu_b[l], w_proj_a[l])
        y_b = _rwkv7_branch(p_rwkv, shift_b[l], w_lora_w[l], w0[l], a_lora_w[l], a0[l],
                            g_lora_w[l], k_k[l], k_a[l], r_k[l], ln_x_w[l], ln_x_b[l],
                            w_proj_b[l])
        gate_a, gate_b = jnp.split(p_gate, 2, axis=-1)
        mixed = jax.nn.sigmoid(gate_a) * y_a + jax.nn.sigmoid(gate_b) * y_b
        h = h + mixed @ w_out[l]
        f = _rms_norm(h, g_ffn[l])
        h = h + jnp.square(jax.nn.relu(f @ w_ffn1[l])) @ w_ffn2[l]
    return _rms_norm(h, g_final)
```

```python
import math
import numpy as np
import concourse.bass as bass
import concourse.mybir as mybir
from concourse.bass_utils import run_bass_kernel_spmd
from contextlib import ExitStack
from collections import deque

F32 = mybir.dt.float32
BF16 = mybir.dt.bfloat16
AF = mybir.ActivationFunctionType
ALU = mybir.AluOpType
AX = mybir.AxisListType

ENGS = ["tensor", "vector", "scalar", "gpsimd", "sync"]

D = 1024
KT = 8
DFF = 4096
CB = 3360
PTOT = 7456
CDEC = math.exp(-0.5)

VC = {}
_o = 0
for _n, _w in [("g_mix", 8), ("sb0", 27), ("sb1", 27), ("w0", 8), ("a0", 8), ("k_k", 8), ("k_a", 8),
               ("ln_x_w", 8), ("ln_x_b", 8), ("g_ffn", 8), ("sgu_ln_w", 8), ("sgu_ln_b", 8), ("r_k", 8)]:
    VC[_n] = _o
    _o += _w
NV = _o


class Buf:
    __slots__ = ("w", "r", "name")

    def __init__(self, name=""):
        self.w = None
        self.r = {}
        self.name = name


class Prog:
    def __init__(self, nc, stack, ndma=32):
        self.nc = nc
        self.sem = {e: stack.enter_context(nc.semaphore("s_" + e)) for e in ENGS}
        self.cnt = {e: 0 for e in ENGS}
        self.q = {e: [] for e in ENGS}
        self.seen = {e: {} for e in ENGS}
        self.evep = {}
        self.clk = {}
        self.locks = {}
        self.dsem = [stack.enter_context(nc.semaphore("d%d" % i)) for i in range(ndma)]
        self.dcnt = [0] * ndma
        self.dpool = {"gpsimd": list(range(0, ndma // 2)), "sync": list(range(ndma // 2, ndma))}
        self.dnext = {"gpsimd": 0, "sync": 0}

    DROP_SAME_WAR = False

    def _waits(self, eng, reads, writes):
        cand = {}
        seen = self.seen[eng]

        def need(ev):
            if ev is None:
                return
            key, sem, val = ev
            if eng == "tensor" and key == "tensor":
                return
            if seen.get(key, 0) >= val:
                return
            if key not in cand or cand[key][1] < val:
                cand[key] = (sem, val)

        for b in reads:
            need(b.w)
        same_ok = self.DROP_SAME_WAR and eng in ("vector", "scalar")
        for b in writes:
            if not (same_ok and b.w is not None and b.w[0] == eng):
                need(b.w)
            for ev in b.r.values():
                if same_ok and ev[0] == eng:
                    continue
                need(ev)
        out = []
        for key, (sem, val) in sorted(cand.items(), key=lambda kv: -kv[1][1]):
            if seen.get(key, 0) >= val:
                continue
            out.append((sem, val))
            seen[key] = val
            for k2, v2 in self.clk.get((key, val), {}).items():
                if seen.get(k2, 0) < v2:
                    seen[k2] = v2
        return out

    skip = False
    caps = None
    defer = None
    epoch = 0

    def mark(self, tag):
        if self.defer is not None:
            self.defer.append(("mark", None, tag, (), (), None))

    def _hop(self, item):
        kind, eng, fn, reads, writes, meta = item
        evs = [b.w for b in reads]
        for b in writes:
            evs += [b.w] + list(b.r.values())
        for ev in evs:
            if ev is None or ev[0] == eng:
                continue
            age = self.epoch - self.evep.get((ev[0], ev[2]), -100)
            if age < (3 if ev[0] == "gpsimd" else 1):
                return True
        return False

    def pump(self, queues, maxops=8):
        self.epoch += 1
        if isinstance(queues, deque):
            queues = [queues]
        caps = self.caps
        for D in queues:
            n = 0
            cnt = {}
            while D and n < maxops:
                if caps and D[0][0] != "mark" and cnt.get(D[0][1], 0) >= caps.get(D[0][1], 99):
                    break
                if D[0][0] == "mark":
                    me = id(D)
                    if D[0][2] == "open":
                        if any(v > 0 for k, v in self.locks.items() if k != me):
                            break
                        self.locks[me] = self.locks.get(me, 0) + 1
                    else:
                        self.locks[me] -= 1
                    D.popleft()
                    continue
                grp = [D[0]]
                meta = D[0][5]
                if meta is not None and not meta[1]:
                    for it in list(D)[1:]:
                        if it[0] == "mark":
                            continue
                        grp.append(it)
                        if it[5] is not None and it[5][1]:
                            break
                if any(self._hop(it) for it in grp):
                    break
                for it in grp:
                    while D[0][0] == "mark":
                        D.rotate(-1)
                    D.popleft()
                    kind, eng, fn, reads, writes, meta = it
                    (self.op if kind == "op" else self.dma)(eng, fn, reads, writes)
                    n += 1
                    cnt[eng] = cnt.get(eng, 0) + 1

    def op(self, eng, fn, reads=(), writes=(), meta=None):
        if self.skip:
            return
        if self.defer is not None:
            self.defer.append(("op", eng, fn, tuple(reads), tuple(writes), meta))
            return
        waits = self._waits(eng, reads, writes)
        self.cnt[eng] += 1
        ev = (eng, self.sem[eng], self.cnt[eng])
        self.evep[(eng, self.cnt[eng])] = self.epoch
        ck = dict(self.seen[eng])
        ck[eng] = self.cnt[eng] - 1
        self.clk[(eng, self.cnt[eng])] = ck
        for b in reads:
            b.r[eng] = ev
        for b in writes:
            b.w = ev
            b.r = {}
        self.q[eng].append((waits, fn, self.sem[eng], 1, meta != "noattach"))

    def dma(self, eng, fn, reads=(), writes=()):
        if self.skip:
            return
        if self.defer is not None:
            self.defer.append(("dma", eng, fn, tuple(reads), tuple(writes), None))
            return
        waits = self._waits(eng, reads, writes)
        pool = self.dpool[eng]
        i = pool[self.dnext[eng] % len(pool)]
        self.dnext[eng] += 1
        prev = 16 * self.dcnt[i]
        if prev > 0 and self.seen[eng].get("d%d" % i, 0) < prev:
            self.seen[eng]["d%d" % i] = prev
            waits.append((self.dsem[i], prev))
        self.dcnt[i] += 1
        ev = ("d%d" % i, self.dsem[i], 16 * self.dcnt[i])
        self.evep[(ev[0], ev[2])] = self.epoch
        self.clk[(ev[0], ev[2])] = dict(self.seen[eng])
        for b in reads:
            b.r[ev[0]] = ev
        for b in writes:
            b.w = ev
            b.r = {}
        self.q[eng].append((waits, fn, self.dsem[i], 16, False))

    def barrier(self):
        evs = [(e, self.sem[e], self.cnt[e]) for e in ENGS if self.cnt[e] > 0]
        evs += [("d%d" % i, self.dsem[i], 16 * self.dcnt[i]) for i in range(len(self.dsem)) if self.dcnt[i] > 0]
        for e in ENGS:
            seen = self.seen[e]
            waits = []
            for key, sem, val in evs:
                if key == e and e == "tensor":
                    continue
                if seen.get(key, 0) < val:
                    seen[key] = val
                    waits.append((sem, val))
            if waits:
                self.q[e].append((waits, None, None, 0, False))

    def final_wait(self, eng, bufs):
        waits = self._waits(eng, bufs, ())
        self.q[eng].append((waits, None, None, 0, False))

    def emit(self, block):
        eng_of = {self.sem[e]: e for e in ENGS}
        waited = {e: set() for e in ENGS}
        for e in ENGS:
            for waits, fn, sem, inc, att in self.q[e]:
                for (s_, v) in waits:
                    if s_ in eng_of:
                        waited[eng_of[s_]].add(v)
        rank = {e: {v: i + 1 for i, v in enumerate(sorted(waited[e]))} for e in ENGS}
        for e in ENGS:
            items = self.q[e]

            def f(eng, items=items, e=e):
                idx = 0
                for waits, fn, sem, inc, att in items:
                    tw = [(s_, rank[eng_of[s_]][v] if s_ in eng_of else v) for (s_, v) in waits]
                    ride = None
                    if att and tw and fn is not None and e in ("vector", "scalar", "gpsimd", "tensor") and sem in eng_of:
                        ride = tw.pop()
                    for (s_, v) in tw:
                        eng.wait_ge(s_, v)
                    if fn is not None:
                        ins = fn(eng)
                        if ride is not None:
                            ins._wait_ge(ride[0], ride[1])
                        if sem in eng_of:
                            idx += 1
                            if idx in waited[e]:
                                ins.then_inc(sem, inc)
                        else:
                            ins.then_inc(sem, inc)

            getattr(block, e)(f)


def build_nc(T, do_a=True, do_b=True, dbg=None, bstop=99):
    NT = T // 128
    SA = min(512, T)
    NSA = T // SA
    TPS = SA // 128
    nc = bass.Bass("TRN2", target_bir_lowering=False)
    dram = lambda n, s, k="ExternalInput": nc.dram_tensor(n, s, F32, kind=k).ap()
    x_d = dram("x", [T, D])
    win_d = dram("w_in", [D, PTOT])
    wpa_d = dram("w_proj_a", [D, D])
    wpb_d = dram("w_proj_b", [D, D])
    wout_d = dram("w_out", [D, D])
    w1_d = dram("w_ffn1", [D, DFF])
    w2_d = dram("w_ffn2", [DFF, D])
    lw_d = dram("w_lora_w", [64, D])
    la_d = dram("a_lora_w", [64, D])
    lg_d = dram("g_lora_w", [160, D])
    swT_d = dram("sgu_wT", [128, 8 * 128])
    vec_d = dram("vecT", [128, NV])
    row_d = dram("rowB", [128, 2048])
    out_d = dram("out", [T, D], "ExternalOutput")
    dbg_d = {}
    if dbg:
        for n, (s, dt) in dbg.items():
            dbg_d[n] = nc.dram_tensor("dbg_" + n, s, dt, kind="ExternalOutput").ap()

    win_v = win_d.rearrange("(kt p) c -> p kt c", p=128)

    with ExitStack() as es:
        P = Prog(nc, es)
        ARENA_BYTES = 212480
        arena = es.enter_context(nc.sbuf_tensor("arena", [128, ARENA_BYTES // 2], BF16))
        psm = lambda name, shape, dt: es.enter_context(nc.psum_tensor(name, shape, dt))

        def view(off, shape, dt):
            n = 1
            for s_ in shape[1:]:
                n *= s_
            if dt == BF16:
                assert off % 2 == 0
                v = arena[0:shape[0], off // 2: off // 2 + n]
            else:
                assert off % 4 == 0
                v = arena[0:shape[0], off // 2: off // 2 + 2 * n].bitcast(F32)
            if len(shape) == 3:
                v = v.rearrange("p (a b) -> p a b", a=shape[1])
            elif len(shape) == 4:
                v = v.rearrange("p (a b c) -> p a b c", a=shape[1], b=shape[2])
            return v

        class Bump:
            def __init__(self, base, limit):
                self.o = base
                self.limit = limit

            def get(self, shape, dt):
                n = 1
                for s_ in shape[1:]:
                    n *= s_
                nb = n * (2 if dt == BF16 else 4)
                nb = (nb + 63) // 64 * 64
                off = self.o
                self.o += nb
                assert self.o <= self.limit, ("arena overflow", self.o, self.limit)
                return view(off, shape, dt)

        def mm(out, lhsT, rhs, start, stop, r, w):
            P.op("tensor", lambda e: e.matmul(out, lhsT=lhsT, rhs=rhs, start=start, stop=stop), r, w, meta=(start, stop))

        def tr(out, in_, r, w):
            P.op("tensor", lambda e: e.transpose(out, in_, identb), r, w)

        def act(out, in_, func, r, w, bias=None, scale=None, accum=None):
            kw = {}
            if bias is not None:
                kw["bias"] = bias
            if scale is not None:
                kw["scale"] = scale
            if accum is not None:
                kw["accum_out"] = accum
            P.op("scalar", lambda e: e.activation(out=out, in_=in_, func=func, **kw), r, w,
                 meta="noattach" if accum is not None else None)

        def tt(eng, out, in0, in1, op, r, w):
            P.op(eng, lambda e: e.tensor_tensor(out=out, in0=in0, in1=in1, op=op), r, w)

        def ts(eng, out, in0, s1, s2, op0, op1, r, w):
            if s2 is None:
                P.op(eng, lambda e: e.tensor_scalar(out=out, in0=in0, scalar1=s1, scalar2=None, op0=op0), r, w)
            else:
                P.op(eng, lambda e: e.tensor_scalar(out=out, in0=in0, scalar1=s1, scalar2=s2, op0=op0, op1=op1), r, w)

        def stt(out, in0, scalar, in1, op0, op1, r, w):
            P.op("vector", lambda e: e.scalar_tensor_tensor(out=out, in0=in0, scalar=scalar, in1=in1, op0=op0, op1=op1), r, w)

        def cp(eng, out, in_, r, w):
            if eng == "scalar":
                P.op("scalar", lambda e: e.activation(out=out, in_=in_, func=AF.Copy), r, w)
            elif eng == "vector":
                P.op(eng, lambda e: e.tensor_scalar(out=out, in0=in_, scalar1=1.0, scalar2=None, op0=ALU.mult), r, w)
            else:
                P.op(eng, lambda e: e.tensor_copy(out=out, in_=in_), r, w)

        def ld(eng, out, in_, w, r=()):
            P.dma(eng, lambda e: e.dma_start(out=out, in_=in_), r, w)

        def mset(ap, val, w):
            P.op("gpsimd", lambda e: e.memset(ap, val), (), w)

        def bc(ap, shape):
            return ap.unsqueeze(2).to_broadcast(shape)

        def bcm(ap, n):
            return ap.unsqueeze(1).to_broadcast([128, n, 128])

        def v3(ap, a):
            return ap.rearrange("p (a b) -> p a b", a=a)

        def red(out, in_, r, w):
            P.op("vector", lambda e: e.tensor_reduce(out=out, in_=in_, axis=AX.X, op=ALU.add), r, w)

        pw = [psm("pw%d" % i, [128, 1024], F32) for i in range(3)]
        pt = [psm("pt%d" % i, [128, 1024], BF16) for i in range(2)]
        Bpw = [[Buf("pw%d_%d" % (i, h)) for h in range(2)] for i in range(3)]
        Bpt = [Buf("pt%d" % i) for i in range(2)]
        pwc = [0]
        left = [None]
        nrot = [3]

        def next_pw():
            left[0] = None
            i = pwc[0] % nrot[0]
            pwc[0] += 1
            return pw[i], Bpw[i]

        def next_half():
            if left[0] is not None:
                i = left[0]
                left[0] = None
                return pw[i][:, 512:1024], Bpw[i][1]
            i = pwc[0] % nrot[0]
            pwc[0] += 1
            left[0] = i
            return pw[i][:, 0:512], Bpw[i][0]

        ptc = [0]

        def next_pt():
            i = ptc[0] % 2
            ptc[0] += 1
            return pt[i], Bpt[i]

        G = Bump(0, 24576)
        vec = G.get([128, NV], F32)
        Bvec = Buf("vec")
        rowb = G.get([128, 2048], F32)
        Brow = Buf("rowb")
        identb = G.get([128, 128], BF16)
        Bid = Buf("ident")
        cf = G.get([128, 128], F32)
        Bcf = Buf("cf")
        onesb = G.get([128, 128], BF16)
        blk1 = G.get([128, 128], BF16)
        m_us = G.get([128, 128], BF16)
        m_ui = G.get([128, 128], BF16)
        m_ls = G.get([128, 128], BF16)
        Bmask = Buf("masks")
        omk = G.get([128, 8], F32)
        Bomk = Buf("omk")
        eps6 = G.get([128, 4], F32)
        Beps = Buf("eps")
        sml = G.get([128, 64], F32)
        Bsml = Buf("sml")
        xbuf = [G.get([128, D], F32) for i in range(2)]
        Bx = [Buf("xb%d" % i) for i in range(2)]
        xnb = G.get([128, D], BF16)
        Bxnb = Buf("xnb")
        junk = G.get([128, D], BF16)
        Bjunk = Buf("junk")
        MB_OFF = 24576
        AT_OFF = MB_OFF + 32768
        PH_OFF = AT_OFF + 32768
        TOP = ARENA_BYTES
        MB = view(MB_OFF, [128, KT, T], BF16)
        aT = view(AT_OFF, [128, KT, T], BF16)
        BMB = [Buf("mb%d" % i) for i in range(NT)]
        BaT = [Buf("aT%d" % i) for i in range(NT)]

        ld("sync", vec, vec_d, [Bvec])
        ld("sync", rowb, row_d, [Brow])

        def gen_mask(dst, cm, step, cmp_op):
            mset(cf, 1.0, [Bcf])
            P.op("gpsimd", lambda e: e.affine_select(out=cf, in_=cf, pattern=[[step, 128]], compare_op=cmp_op,
                                                     fill=0.0, base=0, channel_multiplier=cm), [Bcf], [Bcf])
            cp("vector", dst, cf, [Bcf], [Bmask])

        mset(cf, 0.0, [Bcf])
        P.op("gpsimd", lambda e: e.affine_select(out=cf, in_=cf, pattern=[[-1, 128]], compare_op=ALU.not_equal,
                                                 fill=1.0, base=0, channel_multiplier=1), [Bcf], [Bcf])
        cp("vector", identb, cf, [Bcf], [Bid])
        gen_mask(m_us, -1, 1, ALU.is_gt)
        gen_mask(m_ui, -1, 1, ALU.is_ge)
        gen_mask(m_ls, 1, -1, ALU.is_gt)
        mset(onesb, 1.0, [Bmask])
        mset(blk1, 0.0, [Bmask])
        mset(blk1[0:64, 0:64], 1.0, [Bmask])
        mset(blk1[64:128, 64:128], 1.0, [Bmask])
        ts("vector", omk, vec[:, VC["k_a"]:VC["k_a"] + 8], -1.0, 1.0, ALU.mult, ALU.add, [Bvec], [Bomk])
        mset(eps6[:, 0:1], 1e-6, [Beps])
        mset(eps6[:, 1:2], 1e-5, [Beps])
        mset(eps6[:, 2:3], 64e-5, [Beps])

        Bsml2 = [Buf("sml0"), Buf("sml1"), Buf("sml2")]
        xnb2, Bxnb2 = [xnb, junk], [Bxnb, Bjunk]
        nrm = [0]

        def rms_rstd(src, Bsrc, scratch=None, Bscr=None):
            k = nrm[0] % 2
            nrm[0] += 1
            if scratch is None:
                scratch, Bscr, k = junk, Bjunk, 2
            sm, Bs = sml[:, 4 * k:4 * k + 4], Bsml2[k]
            act(scratch, src, AF.Square, [Bsrc], [Bscr, Bs], accum=sm[:, 0:1])
            act(sm[:, 1:2], sm[:, 0:1], AF.Ln, [Bs, Beps], [Bs], bias=eps6[:, 0:1], scale=1.0 / D)
            act(sm[:, 2:3], sm[:, 1:2], AF.Exp, [Bs], [Bs], scale=-0.5)
            return sm[:, 2:3], Bs

        def norm_a(src, Bsrc, i):
            xn, Bxn = xnb2[i % 2], Bxnb2[i % 2]
            rstd, Bs = rms_rstd(src, Bsrc, xn, Bxn)
            ts("vector", xn, src, rstd, None, ALU.mult, None, [Bsrc, Bs], [Bxn])

        def norm_b(gcol, dstT, i, Bdst):
            xn, Bxn = xnb2[i % 2], Bxnb2[i % 2]
            ptt, Bp = next_pt()
            for kt in range(KT):
                tr(ptt[:, kt * 128:(kt + 1) * 128], xn[:, kt * 128:(kt + 1) * 128], [Bxn, Bid], [Bp])
            tt("vector", dstT[:, :, i * 128:(i + 1) * 128], v3(ptt, KT),
               bc(vec[:, gcol:gcol + 8], [128, KT, 128]), ALU.mult, [Bp, Bvec], [Bdst])

        def load_w(dst, src_view, Bw, split=2):
            n = dst.shape[1]
            step = n // split
            for s_ in range(split):
                ld("gpsimd", dst[:, s_ * step:(s_ + 1) * step, :], src_view[:, s_ * step:(s_ + 1) * step, :], [Bw])

        def dump(name, ap, Bsrc):
            if name in dbg_d:
                P.dma("sync", lambda e: e.dma_start(out=dbg_d[name], in_=ap), [Bsrc], [Bdbg])

        Bdbg = Buf("dbg")
        Bout = Buf("out")

        if do_b:
            preA = Bump(PH_OFF, TOP)
            Wrp_e = preA.get([128, KT, 1824], BF16)
            lwa_e = preA.get([128, D], BF16)
            lg1_e = preA.get([128, D], BF16)
            lg2_e = preA.get([32, D], BF16)
            BWr_e, BWl_e, Blw_e = Buf("Wrp"), Buf("Wlora"), Buf("lw")
            ld("gpsimd", Wrp_e[:, :, 1536:1824], win_v[:, :, 2048 + 3072:2048 + 3360], [BWl_e])
            for part in range(3):
                ld("gpsimd", Wrp_e[:, :, part * 512:(part + 1) * 512],
                   win_v[:, :, 2048 + part * 1024:2048 + part * 1024 + 512], [BWr_e])
            ld("gpsimd", lwa_e[0:64, :], lw_d, [Blw_e])
            ld("gpsimd", lwa_e[64:128, :], la_d, [Blw_e])
            ld("gpsimd", lg1_e, lg_d[0:128, :], [Blw_e])
            ld("gpsimd", lg2_e, lg_d[128:160, :], [Blw_e])

        for i in range(NT + 1):
            if i < NT:
                xb_, Bxb = xbuf[i % 2], Bx[i % 2]
                ld("sync", xb_, x_d[i * 128:(i + 1) * 128, :], [Bxb])
                norm_a(xb_, Bxb, i)
            if i > 0:
                norm_b(VC["g_mix"], aT, i - 1, BaT[i - 1])
        if "aT" in dbg_d:
            P.dma("sync", lambda e: e.dma_start(out=v3(dbg_d["aT"], KT), in_=aT), BaT, [Bdbg])

        if do_b:
            nrot[0] = 2
            A = Bump(PH_OFF, TOP)
            Wrp = A.get([128, KT, 1824], BF16)
            BWr = BWr_e
            BWl = BWl_e
            lwa = A.get([128, D], BF16)
            lg1 = A.get([128, D], BF16)
            lg2 = A.get([32, D], BF16)
            Blw = Blw_e
            Rk = A.get([128, KT, 128], BF16)
            BRk = Buf("Rk")
            rst = A.get([128, 4, 128], F32)
            Brst = Buf("rst")
            carry = A.get([128, 16], F32)
            Bcar = Buf("carry")
            q2 = [A.get([128, 12, 128], BF16) for _ in range(2)]
            Bq2 = [[Buf("q%d_%d" % (s_, i)) for i in range(3)] for s_ in range(2)]
            lq = A.get([128, 2, 128], F32)
            lq2 = A.get([32, 128], F32)
            Blq = Buf("lq")
            tmp4 = A.get([128, 4, 128], F32)
            Btmp4 = Buf("tmp4")
            t2 = A.get([128, 4, 128], F32)
            Bt2 = Buf("t2")
            lb = A.get([128, 128], BF16)
            sg1 = A.get([128, 128], BF16)
            sg2 = A.get([32, 128], BF16)
            Blb = Buf("lb")
            f4 = lambda: A.get([128, 4, 128], F32)
            b4 = lambda: A.get([128, 4, 128], BF16)
            sgw, cs, E1, E2, aa, kkf, sq, kpf = f4(), f4(), f4(), f4(), f4(), f4(), f4(), f4()
            Bsgw, Bcs, BE1, BE2, Baa, Bkkf, Bsq, Bkpf = [Buf(n) for n in "sgw cs E1 E2 aa kkf sq kpf".split()]
            kk2, rk_ = b4(), b4()
            Bkk2, Brk = Buf("kk2"), Buf("rk")
            SP = Bump(11200, 23488)
            at2, bt2, kt2 = [[b4(), b4()] for _ in range(3)]
            Bat2, Bbt2, Bkt2 = [[Buf(n + "0"), Buf(n + "1")] for n in "at bt kt".split()]
            rt2, gT2, bon2 = [[b4(), b4(), SP.get([128, 4, 128], BF16)] for _ in range(3)]
            Brt2, BgT2, Bbon2 = [[Buf(n + "0"), Buf(n + "1"), Buf(n + "2")] for n in "rt gT bon".split()]
            Wc2 = [A.get([128, 4], F32) for _ in range(3)]
            BWc2 = [Buf("Wc0"), Buf("Wc1"), Buf("Wc2")]
            m8 = lambda: A.get([128, 8, 128], BF16)
            Xs, Ls, Ys_ = [m8(), m8()], [m8(), m8()], [m8(), m8()]
            BXh = [[Buf("X%d%d" % (a, h)) for h in range(2)] for a in range(2)]
            BLh = [[Buf("L%d%d" % (a, h)) for h in range(2)] for a in range(2)]
            BYh_ = [[Buf("Y%d%d" % (a, h)) for h in range(2)] for a in range(2)]
            Ys1b = SP.get([128, 8, 128], BF16)
            BYh1b = [Buf("Y1b0"), Buf("Y1b1")]
            AkT = m8()
            BAk = Buf("AkT")
            ArbT2, ArkT2 = [m8(), SP.get([128, 8, 128], BF16)], [m8(), SP.get([128, 8, 128], BF16)]
            BArb2, BArk2 = [Buf("ArbT0"), Buf("ArbT1")], [Buf("ArkT0"), Buf("ArkT1")]
            btT2 = [A.get([128, 512], BF16), SP.get([128, 512], BF16)]
            ktT2 = [A.get([128, 512], BF16), SP.get([128, 512], BF16)]
            vT2 = [A.get([128, 512], BF16), SP.get([128, 512], BF16)]
            BbtT2, BktT2, BvT2 = [[Buf(n + "0"), Buf(n + "1")] for n in "btT ktT vT".split()]
            GT2 = [A.get([128, 4, 128], BF16), A.get([128, 4, 128], BF16)]
            BGT2 = [Buf("GT0"), Buf("GT1")]
            U = A.get([128, 8, 64], BF16)
            BU = Buf("U")
            ST = A.get([128, 4, 128], F32)
            STt = A.get([128, 4, 128], F32)
            STb = A.get([128, 4, 128], BF16)
            BST, BSTt, BSTb = Buf("ST"), Buf("STt"), Buf("STb")
            osq = A.get([128, 8, 64], F32)
            Bosq = Buf("osq")
            otmp = A.get([128, 8, 64], F32)
            Botmp = Buf("otmp")
            onb = A.get([128, 8, 64], BF16)
            Bonb = Buf("onb")
            gst = A.get([128, 64], F32)
            Bgst = Buf("gst")
            o2 = [A.get([128, 128], F32) for _ in range(2)]
            Bo2 = [Buf("o2a"), Buf("o2b")]
            ocp = A.get([128, 8, 64], F32)
            Bocp = Buf("ocp")
            flat = lambda ap: ap.rearrange("p a b -> p (a b)")
            pbc = [0]

            def next_pb():
                i = pbc[0] % 2
                pbc[0] += 1
                return pw[2][:, i * 512:(i + 1) * 512], Bpw[2][i]

            for f in range(KT):
                ts("vector", Rk[:, f, :], blk1, vec[:, VC["r_k"] + f:VC["r_k"] + f + 1], None, ALU.mult, None,
                   [Bmask, Bvec], [BRk])
            mset(rst, 1.0, [Brst])
            mset(rst[:, :, 0:1], 0.0, [Brst])

            def b12(hg, c):
                si_ = hg * NT + c
                sl, s3 = si_ % 2, si_ % 3
                q, Bq = q2[sl], Bq2[sl]
                at_, bt_, kt_, rt_, gT, bon, Wc = at2[sl], bt2[sl], kt2[sl], rt2[s3], gT2[s3], bon2[s3], Wc2[s3]
                Bat, Bbt, Bkt, Brt, BgT, Bbon, BWc = Bat2[sl], Bbt2[sl], Bkt2[sl], Brt2[s3], BgT2[s3], Bbon2[s3], BWc2[s3]
                tok = slice(c * 128, (c + 1) * 128)
                if c == 0:
                    if hg > 0:
                        for part in range(3):
                            ld("gpsimd", Wrp[:, :, part * 512:(part + 1) * 512],
                               win_v[:, :, 2048 + part * 1024 + hg * 512:2048 + part * 1024 + (hg + 1) * 512], [BWr])
                    mset(carry, 0.0, [Bcar])
                for bi in range(3):
                    j0 = bi * 8 + hg * 4
                    pq, Bpq = next_pb()
                    P.mark("open")
                    for fi in range(4):
                        for kt in range(KT):
                            mm(pq[:, fi * 128:(fi + 1) * 128], Wrp[:, kt, bi * 512 + fi * 128:bi * 512 + (fi + 1) * 128],
                               aT[:, kt, tok], kt == 0, kt == KT - 1, [BWr, BaT[c]], [Bpq])
                    pq3 = v3(pq, 4)
                    tt("vector", tmp4, pq3, bc(vec[:, VC["sb1"] + j0:VC["sb1"] + j0 + 4], [128, 4, 128]), ALU.mult,
                       [Bpq, Bvec], [Btmp4])
                    tt("vector", t2, pq3, bc(vec[:, VC["sb0"] + j0:VC["sb0"] + j0 + 4], [128, 4, 128]), ALU.mult,
                       [Bpq, Bvec], [Bt2])
                    P.mark("close")
                    qv = q[:, bi * 4:(bi + 1) * 4, :]
                    tt("vector", qv[:, :, 1:128], t2[:, :, 1:128], tmp4[:, :, 0:127], ALU.add, [Bt2, Btmp4], [Bq[bi]])
                    tt("vector", qv[:, :, 0:1], t2[:, :, 0:1], carry[:, bi * 4:(bi + 1) * 4].unsqueeze(2), ALU.add,
                       [Bt2, Bcar], [Bq[bi]])
                    ts("vector", carry[:, bi * 4:(bi + 1) * 4].unsqueeze(2), tmp4[:, :, 127:128], 1.0, None, ALU.mult, None,
                       [Btmp4], [Bcar])
                pq, Bpq = next_pb()
                P.mark("open")
                for li, (c0, c1, rows) in enumerate([(1536, 1664, 128), (1664, 1792, 128), (1792, 1824, 32)]):
                    for kt in range(KT):
                        mm(pq[0:rows, li * 128:(li + 1) * 128], Wrp[:, kt, c0:c1], aT[:, kt, tok], kt == 0, kt == KT - 1,
                           [BWl, BaT[c]], [Bpq])
                for li, rows, dst in [(0, 128, lq[:, 0, :]), (1, 128, lq[:, 1, :]), (2, 32, lq2)]:
                    jc = 24 + li
                    pql = pq[0:rows, li * 128:(li + 1) * 128]
                    s1 = vec[0:rows, VC["sb1"] + jc:VC["sb1"] + jc + 1]
                    s0 = vec[0:rows, VC["sb0"] + jc:VC["sb0"] + jc + 1]
                    tl = tmp4[0:rows, 0, :]
                    ts("vector", tl, pql, s1, None, ALU.mult, None, [Bpq, Bvec], [Btmp4])
                    stt(dst[:, 1:128], pql[:, 1:128], s0, tl[:, 0:127], ALU.mult, ALU.add, [Bpq, Bvec, Btmp4], [Blq])
                    stt(dst[:, 0:1], pql[:, 0:1], s0, carry[0:rows, 12 + li:13 + li], ALU.mult, ALU.add,
                        [Bpq, Bvec, Bcar], [Blq])
                    ts("vector", carry[0:rows, 12 + li:13 + li], tl[:, 127:128], 1.0, None, ALU.mult, None, [Btmp4], [Bcar])
                P.mark("close")
                act(lb[0:64, :], lq[0:64, 0, :], AF.Tanh, [Blq], [Blb])
                cp("vector", lb[64:128, :], lq[64:128, 0, :], [Blq], [Blb])
                act(sg1, lq[:, 1, :], AF.Sigmoid, [Blq], [Blb])
                act(sg2, lq2, AF.Sigmoid, [Blq], [Blb])
                pz, Bpz = next_pb()
                pa, Bpa = next_pb()
                P.mark("open")
                for fi in range(4):
                    f = hg * 4 + fi
                    fs = slice(f * 128, (f + 1) * 128)
                    cs_ = slice(fi * 128, (fi + 1) * 128)
                    mm(pz[:, cs_], lwa[0:64, fs], lb[0:64, :], True, True, [Blw, Blb], [Bpz])
                    mm(pa[:, cs_], lwa[64:128, fs], lb[64:128, :], True, True, [Blw, Blb], [Bpa])
                for fi in range(4):
                    f = hg * 4 + fi
                    cs_ = slice(fi * 128, (fi + 1) * 128)
                    act(sgw[:, fi, :], pz[:, cs_], AF.Sigmoid, [Bpz, Bvec], [Bsgw],
                        bias=vec[:, VC["w0"] + f:VC["w0"] + f + 1])
                    act(aa[:, fi, :], pa[:, cs_], AF.Sigmoid, [Bpa, Bvec], [Baa],
                        bias=vec[:, VC["a0"] + f:VC["a0"] + f + 1])
                P.mark("close")
                pg, Bpg = next_pb()
                P.mark("open")
                for fi in range(4):
                    f = hg * 4 + fi
                    fs = slice(f * 128, (f + 1) * 128)
                    cs_ = slice(fi * 128, (fi + 1) * 128)
                    mm(pg[:, cs_], lg1[:, fs], sg1, True, False, [Blw, Blb], [Bpg])
                    mm(pg[:, cs_], lg2[0:32, fs], sg2[0:32, :], False, True, [Blw, Blb], [Bpg])
                cp("scalar", gT, v3(pg, 4), [Bpg], [BgT])
                P.mark("close")
                P.op("vector", lambda e: e.tensor_tensor_scan(out=flat(cs), data0=flat(rst), data1=flat(sgw), initial=0.0,
                                                              op0=ALU.mult, op1=ALU.add), [Brst, Bsgw], [Bcs])
                tt("gpsimd", sgw, cs, sgw, ALU.subtract, [Bcs, Bsgw], [Bsgw])
                act(E1, cs, AF.Exp, [Bcs], [BE1], scale=-CDEC)
                act(E2, cs, AF.Exp, [Bcs], [BE2], scale=CDEC)
                act(sgw, sgw, AF.Exp, [Bsgw], [Bsgw], scale=-CDEC)
                ts("vector", Wc[:].unsqueeze(2), E1[:, :, 127:128], 1.0, None, ALU.mult, None, [BE1], [BWc])
                qr, qk, qv_ = q[:, 0:4, :], q[:, 4:8, :], q[:, 8:12, :]
                kcol = VC["k_k"] + hg * 4
                tt("vector", kkf, qk, bc(vec[:, kcol:kcol + 4], [128, 4, 128]), ALU.mult, [Bq[1], Bvec], [Bkkf])
                act(kk2, kkf, AF.Square, [Bkkf], [Bkk2])
                pss, Bpss = next_pb()
                P.mark("open")
                mm(pss, blk1, flat(kk2), True, True, [Bmask, Bkk2], [Bpss])
                ts("vector", sq, v3(pss, 4), 1e-24, None, ALU.max, None, [Bpss], [Bsq])
                P.mark("close")
                act(sq, sq, AF.Ln, [Bsq], [Bsq])
                act(sq, sq, AF.Exp, [Bsq], [Bsq], scale=-0.5)
                tt("vector", kkf, kkf, sq, ALU.mult, [Bkkf, Bsq], [Bkkf])
                stt(at_, kkf, -1.0, sgw, ALU.mult, ALU.mult, [Bkkf, Bsgw], [Bat])
                tt("gpsimd", sq, kkf, aa, ALU.mult, [Bkkf, Baa], [Bsq])
                tt("gpsimd", bt_, sq, E2, ALU.mult, [Bsq, BE2], [Bbt])
                acol = VC["k_a"] + hg * 4
                tt("vector", aa, aa, bc(vec[:, acol:acol + 4], [128, 4, 128]), ALU.mult, [Baa, Bvec], [Baa])
                tt("vector", aa, aa, bc(omk[:, hg * 4:hg * 4 + 4], [128, 4, 128]), ALU.add, [Baa, Bomk], [Baa])
                tt("vector", kpf, qk, aa, ALU.mult, [Bq[1], Baa], [Bkpf])
                tt("gpsimd", kt_, kpf, E2, ALU.mult, [Bkpf, BE2], [Bkt])
                tt("vector", rk_, qr, kpf, ALU.mult, [Bq[0], Bkpf], [Brk])
                tt("vector", rt_, qr, E1, ALU.mult, [Bq[0], BE1], [Brt])
                pbon, Bpbon = next_pb()
                P.mark("open")
                for fi in range(4):
                    f = hg * 4 + fi
                    mm(pbon[:, fi * 128:(fi + 1) * 128], Rk[:, f, :], rk_[:, fi, :], True, True, [BRk, Brk], [Bpbon])
                tt("vector", bon, v3(pbon, 4), qv_, ALU.mult, [Bpbon, Bq[2]], [Bbon])
                P.mark("close")

            def b3(hg, c, pump_):
                si_ = hg * NT + c
                sl, s3 = si_ % 2, si_ % 3
                q, Bq = q2[sl], Bq2[sl]
                at_, bt_, kt_, rt_, gT, bon, Wc = at2[sl], bt2[sl], kt2[sl], rt2[s3], gT2[s3], bon2[s3], Wc2[s3]
                Bat, Bbt, Bkt, Brt, BgT, Bbon, BWc = Bat2[sl], Bbt2[sl], Bkt2[sl], Brt2[s3], BgT2[s3], Bbon2[s3], BWc2[s3]
                GT, BGT = GT2[sl], BGT2[sl]
                btT, ktT, vT, BbtT, BktT, BvT = btT2[sl], ktT2[sl], vT2[sl], BbtT2[sl], BktT2[sl], BvT2[sl]
                ArbT, ArkT, BArb, BArk = ArbT2[sl], ArkT2[sl], BArb2[sl], BArk2[sl]
                Ys = [Ys_[0], Ys_[1] if sl == 0 else Ys1b]
                BYh = [BYh_[0], BYh_[1] if sl == 0 else BYh1b]
                qv_ = q[:, 8:12, :]
                tok = slice(c * 128, (c + 1) * 128)

                def pump():
                    if P.defer is None:
                        pump_()
                ptA, BptA = next_pt()
                ptB, BptB = next_pt()
                for fi in range(4):
                    cs_ = slice(fi * 128, (fi + 1) * 128)
                    cs2 = slice(512 + fi * 128, 512 + (fi + 1) * 128)
                    tr(ptA[:, cs_], at_[:, fi, :], [Bat, Bid], [BptA])
                    tr(ptA[:, cs2], bt_[:, fi, :], [Bbt, Bid], [BptA])
                    tr(ptB[:, cs_], kt_[:, fi, :], [Bkt, Bid], [BptB])
                    tr(ptB[:, cs2], qv_[:, fi, :], [Bq[2], Bid], [BptB])
                cp("vector", Ys[0][:, :, 0:64], v3(ptA[:, 0:512], 8), [BptA], BYh[0])
                cp("vector", btT, ptA[:, 512:1024], [BptA], [BbtT])
                cp("vector", ktT, ptB[:, 0:512], [BptB], [BktT])
                cp("vector", vT, ptB[:, 512:1024], [BptB], [BvT])
                pump()

                def amat(lh, Blh, rh, Brh, dst, Bdst, mask):
                    pk, Bpk = next_pw()
                    dst4 = dst.rearrange("q (f p) n -> q f p n", p=2)
                    for p in range(2):
                        ps_ = slice(p * 64, (p + 1) * 64)
                        for fi in range(4):
                            mm(pk[:, p * 512 + fi * 128:p * 512 + (fi + 1) * 128], lh[ps_, fi, :], rh[ps_, fi, :],
                               True, True, [Blh, Brh], [Bpk[p]])
                    pump()
                    for p in range(2):
                        tt("vector", dst4[:, :, p, :], v3(pk[:, p * 512:(p + 1) * 512], 4), bcm(mask, 4),
                           ALU.mult, [Bpk[p], Bmask], Bdst)
                    pump()

                amat(kt_, Bkt, at_, Bat, AkT, [BAk], m_us)
                amat(bt_, Bbt, at_, Bat, Xs[0], BXh[0], m_us)
                amat(at_, Bat, bt_, Bbt, Ls[0], BLh[0], m_ls)
                pwv, Bpwv = next_half()
                for hl in range(8):
                    mm(pwv[:, hl * 64:(hl + 1) * 64], AkT[:, hl, :], vT[:, hl * 64:(hl + 1) * 64], True, True,
                       [BAk, BvT], [Bpwv])
                cp("scalar", Ys[0][:, :, 64:128], v3(pwv, 8), [Bpwv], BYh[0])
                amat(bt_, Bbt, rt_, Brt, ArbT, [BArb], m_ui)
                amat(kt_, Bkt, rt_, Brt, ArkT, [BArk], m_ui)
                for lv in range(7):
                    a_, b_ = lv % 2, (lv + 1) % 2
                    for hb in range(2):
                        hs = slice(hb * 4, (hb + 1) * 4)
                        pY, BpY = next_half()
                        for j in range(4):
                            hl = hb * 4 + j
                            oy = pY[:, j * 128:(j + 1) * 128]
                            mm(oy, Xs[a_][:, hl, :], Ys[a_][:, hl, :], True, False, [BXh[a_][hb], BYh[a_][hb]], [BpY])
                            mm(oy, identb, Ys[a_][:, hl, :], False, True, [Bid, BYh[a_][hb]], [BpY])
                        pump()
                        if lv < 6:
                            pX, BpX = next_half()
                            for j in range(4):
                                hl = hb * 4 + j
                                mm(pX[:, j * 128:(j + 1) * 128], Ls[a_][:, hl, :], Xs[a_][:, hl, :], True, True,
                                   [BLh[a_][hb], BXh[a_][hb]], [BpX])
                            pump()
                        cp("scalar", Ys[b_][:, hs, :], v3(pY, 4), [BpY], [BYh[b_][hb]])
                        if lv < 5:
                            pL, BpL = next_half()
                            for j in range(4):
                                hl = hb * 4 + j
                                mm(pL[:, j * 128:(j + 1) * 128], Xs[a_][:, hl, :], Ls[a_][:, hl, :], True, True,
                                   [BXh[a_][hb], BLh[a_][hb]], [BpL])
                            pump()
                        if lv < 6:
                            cp("vector", Xs[b_][:, hs, :], v3(pX, 4), [BpX], [BXh[b_][hb]])
                        if lv < 5:
                            cp("scalar", Ls[b_][:, hs, :], v3(pL, 4), [BpL], [BLh[b_][hb]])
                        pump()
                Yf, BYf = Ys[1], BYh[1]
                ptG, BptG = next_pt()
                for hl in range(8):
                    fi, p = hl // 2, hl % 2
                    tr(ptG[p * 64:(p + 1) * 64, fi * 128:(fi + 1) * 128], Yf[:, hl, 0:64], BYf + [Bid], [BptG])
                cp("vector", GT, v3(ptG[:, 0:512], 4), [BptG], [BGT])
                pump()
                P.defer = DT
                if c == 0:
                    mset(ST, 0.0, [BST])
                    mset(STb, 0.0, [BSTb])
                P.mark("open")
                pU, BpU = pw[2], Bpw[2]
                U4 = U.rearrange("q (f p) n -> q f p n", p=2)
                for p in range(2):
                    ps_ = slice(p * 64, (p + 1) * 64)
                    for fi in range(4):
                        hl = 2 * fi + p
                        ou = pU[:, p * 512 + fi * 64:p * 512 + (fi + 1) * 64]
                        mm(ou, GT[ps_, fi, :], STb[ps_, fi, p * 64:(p + 1) * 64], True, False, [BGT, BSTb], [BpU[p]])
                        mm(ou, identb, Yf[:, hl, 64:128], False, True, [Bid] + BYf, [BpU[p]])
                for p in range(2):
                    cp("scalar", U4[:, :, p, :], v3(pU[:, p * 512:p * 512 + 256], 4), [BpU[p]], [BU])
                pO, BpO = pw[2], Bpw[2]
                for p in range(2):
                    ps_ = slice(p * 64, (p + 1) * 64)
                    for fi in range(4):
                        hl = 2 * fi + p
                        oo = pO[:, p * 512 + fi * 64:p * 512 + (fi + 1) * 64]
                        mm(oo, rt_[ps_, fi, :], STb[ps_, fi, p * 64:(p + 1) * 64], True, False, [Brt, BSTb], [BpO[p]])
                        mm(oo, ArbT[:, hl, :], U[:, hl, :], False, False, [BArb, BU], [BpO[p]])
                        mm(oo, ArkT[:, hl, :], vT[:, hl * 64:(hl + 1) * 64], False, True, [BArk, BvT], [BpO[p]])
                for fi in range(4):
                    cs_ = slice(fi * 128, (fi + 1) * 128)
                    os_ = pw[2][:, (fi // 2) * 512 + 256 + (fi % 2) * 128:(fi // 2) * 512 + 256 + (fi % 2 + 1) * 128]
                    mm(os_, btT[:, cs_], U[:, 2 * fi:2 * fi + 2, :].rearrange("p a b -> p (a b)"), True, False,
                       [BbtT, BU], [Bpw[2][fi // 2]])
                    mm(os_, ktT[:, cs_], vT[:, cs_], False, True, [BktT, BvT], [Bpw[2][fi // 2]])
                for b_ in range(2):
                    tt("vector", STt[:, 2 * b_:2 * b_ + 2, :], v3(pw[2][:, b_ * 512 + 256:b_ * 512 + 512], 2),
                       ST[:, 2 * b_:2 * b_ + 2, :], ALU.add, [Bpw[2][b_], BST], [BSTt])
                tt("vector", ST, STt, bc(Wc, [128, 4, 128]), ALU.mult, [BSTt, BWc], [BST])
                cp("scalar", STb, ST, [BST], [BSTb])
                cp("scalar", ocp[:, 0:4, :], v3(pO[:, 0:256], 4), [BpO[0]], [Bocp])
                cp("vector", ocp[:, 4:8, :], v3(pO[:, 512:768], 4), [BpO[1]], [Bocp])
                P.mark("close")
                red(gst[:, 0:8], ocp, [Bocp], [Bgst])
                act(osq, ocp, AF.Square, [Bocp], [Bosq])
                red(gst[:, 8:16], osq, [Bosq], [Bgst])
                ts("vector", gst[:, 16:24], gst[:, 0:8], 1.0 / 64, None, ALU.mult, None, [Bgst], [Bgst])
                tt("vector", gst[:, 24:32], gst[:, 16:24], gst[:, 16:24], ALU.mult, [Bgst], [Bgst])
                stt(gst[:, 32:40], gst[:, 8:16], 1.0 / 64, gst[:, 24:32], ALU.mult, ALU.subtract, [Bgst], [Bgst])
                act(gst[:, 40:48], gst[:, 32:40], AF.Ln, [Bgst, Beps], [Bgst], bias=eps6[:, 2:3])
                act(gst[:, 48:56], gst[:, 40:48], AF.Exp, [Bgst], [Bgst], scale=-0.5)
                tt("vector", otmp, ocp, bc(gst[:, 16:24], [128, 8, 64]), ALU.subtract, [Bocp, Bgst], [Botmp])
                onb4 = onb.rearrange("q (f p) n -> q p f n", p=2)
                tt("vector", onb4, otmp.rearrange("q (p f) n -> q p f n", p=2),
                   gst[:, 48:56].rearrange("q (p f) -> q p f", p=2).unsqueeze(3).to_broadcast([128, 2, 4, 64]),
                   ALU.mult, [Botmp, Bgst], [Bonb])
                ptO, BptO = next_pt()
                for fi in range(4):
                    tr(ptO[:, fi * 128:(fi + 1) * 128], onb[:, 2 * fi:2 * fi + 2, :].rearrange("p a b -> p (a b)"),
                       [Bonb, Bid], [BptO])
                for fi in range(4):
                    f = hg * 4 + fi
                    stt(o2[fi % 2], ptO[:, fi * 128:(fi + 1) * 128], vec[:, VC["ln_x_w"] + f:VC["ln_x_w"] + f + 1], bon[:, fi, :],
                        ALU.mult, ALU.add, [BptO, Bvec, Bbon], [Bo2[fi % 2]])
                    stt(MB[:, f, tok], o2[fi % 2], vec[:, VC["ln_x_b"] + f:VC["ln_x_b"] + f + 1], gT[:, fi, :],
                        ALU.add, ALU.mult, [Bo2[fi % 2], Bvec, BgT], [BMB[c]])
                P.defer = None

            DT = deque()
            steps = [(hg, c) for hg in range(2) for c in range(NT)]
            b12(*steps[0])
            P.barrier()

            def mkpump(DQ):
                def pump():
                    P.caps = {"tensor": 8, "vector": 2, "scalar": 2, "gpsimd": 1}
                    P.pump([DT, DQ], 10)
                    P.caps = None
                return pump

            for si, (hg, c) in enumerate(steps):
                DQ = deque()
                if si + 1 < len(steps):
                    P.defer = DQ
                    b12(*steps[si + 1])
                    P.defer = None
                b3(hg, c, mkpump(DQ))
                while DQ:
                    n0 = len(DQ)
                    P.pump(DQ, 1000)
                    if len(DQ) == n0:
                        P.pump(DT, 8)
            while DT:
                P.pump(DT, 1000)
            nrot[0] = 3
            if "obT" in dbg_d:
                P.dma("sync", lambda e: e.dma_start(out=v3(dbg_d["obT"], KT), in_=MB), BMB, [Bdbg])
        else:
            for c in range(NT):
                mset(MB[:, :, c * 128:(c + 1) * 128], 0.0, [BMB[c]])
        P.barrier()

        A1W = Bump(PH_OFF, PH_OFF + 65536)
        Wu_, Wv_, Wg1_, Wp1_ = [A1W.get([128, KT, D], BF16) for _ in range(4)]
        BWu, BWv, BWg1, BWp1 = Buf("Wu"), Buf("Wv"), Buf("Wg1"), Buf("Wp1")
        A = Bump(PH_OFF + 65536, PH_OFF + 65536 + 32768)
        Wg = A.get([128, KT, D], BF16)
        Wp = A.get([128, KT, D], BF16)
        BWg, BWp = Buf("Wg"), Buf("Wp")
        WO_OFF = TOP - 16384
        Wo = view(WO_OFF, [128, KT, D], BF16)
        BWo = Buf("Wo")
        SPg = Bump(11200, 23488)
        stage = SPg.get([128, KT, SA], BF16)
        Bstage = Buf("stage")
        sgt = [SPg.get([128, SA], BF16) for _ in range(2)]
        Bsgt = [Buf("sgt0"), Buf("sgt1")]
        BWp8 = [Buf("Wp2_%d" % m) for m in range(KT)]
        BWg8 = [Buf("Wg2_%d" % m) for m in range(KT)]
        wpb_v = wpb_d.rearrange("(kt p) c -> p kt c", p=128)
        for m in range(0, KT, 2):
            ms2 = slice(m * 128, (m + 2) * 128)
            ld("gpsimd", Wp[:, :, ms2], wpb_v[:, :, ms2], [BWp8[m], BWp8[m + 1]])
            ld("gpsimd", Wg[:, :, ms2], win_v[:, :, 2048 + CB + D + m * 128:2048 + CB + D + (m + 2) * 128], [BWg8[m], BWg8[m + 1]])
        if do_a:
            load_w(Wu_, win_v[:, :, 0:D], BWu)
            load_w(Wv_, win_v[:, :, D:2 * D], BWv)
            load_w(Wp1_, wpa_d.rearrange("(kt p) c -> p kt c", p=128), BWp1)
            load_w(Wg1_, win_v[:, :, 2048 + CB:2048 + CB + D], BWg1)
        load_w(Wo, wout_d.rearrange("(kt p) c -> p kt c", p=128), BWo)
        for s_ in range(NSA):
            stok = slice(s_ * SA, (s_ + 1) * SA)
            tiles = list(range(s_ * TPS, (s_ + 1) * TPS))
            for m in range(KT):
                ms = slice(m * 128, (m + 1) * 128)
                py, Bpy = next_half()
                pg, Bpg = next_half()
                for f in range(KT):
                    mm(py[:, 0:SA], Wp[:, f, ms], MB[:, f, stok], f == 0, f == KT - 1, [BWp8[m]] + [BMB[t] for t in tiles], [Bpy])
                for kt in range(KT):
                    mm(pg[:, 0:SA], Wg[:, kt, ms], aT[:, kt, stok], kt == 0, kt == KT - 1, [BWg8[m]] + [BaT[t] for t in tiles], [Bpg])
                act(sgt[m % 2], pg[:, 0:SA], AF.Sigmoid, [Bpg], [Bsgt[m % 2]])
                tt("vector", stage[:, m, :], py[:, 0:SA], sgt[m % 2], ALU.mult, [Bpy, Bsgt[m % 2]], [Bstage])
            cp("vector", MB[:, :, stok], stage, [Bstage], [BMB[t] for t in tiles])
        P.barrier()

        if do_a:
            A = Bump(PH_OFF + 65536, WO_OFF)
            Wu, Wv, Wg, Wp = Wu_, Wv_, Wg1_, Wp1_
            BWg, BWp = BWg1, BWp1
            wsf = A.get([128, 8, 128], F32)
            wsT = A.get([128, 8, 128], BF16)
            Bws = Buf("ws")
            Cst = A.get([128, 8, 128], F32)
            BCst = Buf("Cst")
            uT = A.get([128, KT, SA], BF16)
            BuT = Buf("uT")
            sT = A.get([128, KT, SA], BF16)
            BsT = [Buf("sT%d" % i) for i in range(TPS)]
            vg = A.get([128, D], F32)
            Bvg = Buf("vg")
            vn = A.get([128, D], BF16)
            Bvn = Buf("vn")
            bst = A.get([128, 16], F32)
            Bbst = Buf("bst")
            stmp = [A.get([128, 128], F32) for _ in range(2)]
            Bstmp = [Buf("stmp0"), Buf("stmp1")]
            sgt = [A.get([128, SA], BF16) for _ in range(2)]
            Bsgt = [Buf("sgt0"), Buf("sgt1")]
            mtmp = [A.get([128, SA], BF16) for _ in range(2)]
            Bmtmp = [Buf("mtmp0"), Buf("mtmp1")]
            SPa = Bump(11200, 23488)
            vg2, vn2, bst2 = [vg, SPa.get([128, D], F32)], [vn, A.get([128, D], BF16)], [bst, A.get([128, 16], F32)]
            Bvg2, Bvn2, Bbst2 = [Bvg, Buf("vg1")], [Bvn, Buf("vn1")], [Bbst, Buf("bst1")]
            sT2 = [sT, SPa.get([128, KT, SA], BF16)]
            BsT2 = [BsT, [Buf("sTb%d" % i) for i in range(TPS)]]
            DA = deque()
            nrot[0] = 2
            ld("sync", wsf.rearrange("p a b -> p (a b)"), swT_d, [Bws])
            tt("vector", wsT, wsf, bcm(m_ui, 8), ALU.mult, [Bws, Bmask], [Bws])
            for g in range(8):
                prs, Bprs = next_half()
                mm(prs[:, 0:128], onesb, wsT[:, g, :], True, True, [Bmask, Bws], [Bprs])
                stt(Cst[:, g, :], prs[:, 0:128], vec[:, VC["sgu_ln_b"] + g:VC["sgu_ln_b"] + g + 1],
                    rowb[:, 1024 + g * 128:1024 + (g + 1) * 128], ALU.mult, ALU.add, [Bprs, Bvec, Brow], [BCst])
            for s_ in range(NSA):
                stok = slice(s_ * SA, (s_ + 1) * SA)
                tiles = list(range(s_ * TPS, (s_ + 1) * TPS))
                RaT = [BaT[t] for t in tiles]
                sT, BsT = sT2[s_ % 2], BsT2[s_ % 2]
                for f in range(KT):
                    pu, Bpu = next_half()
                    for kt in range(KT):
                        mm(pu[:, 0:SA], Wu[:, kt, f * 128:(f + 1) * 128], aT[:, kt, stok], kt == 0, kt == KT - 1,
                           [BWu] + RaT, [Bpu])
                    act(uT[:, f, :], pu[:, 0:SA], AF.Gelu, [Bpu], [BuT])
                def sgu1(ti, t):
                    k = ti % 2
                    vg_, vn_, bst_, Bvg_, Bvn_, Bbst_ = vg2[k], vn2[k], bst2[k], Bvg2[k], Bvn2[k], Bbst2[k]
                    tok = slice(t * 128, (t + 1) * 128)
                    for half in range(2):
                        pv, Bpv = next_half()
                        for kt in range(KT):
                            mm(pv, aT[:, kt, tok], Wv[:, kt, half * 512:(half + 1) * 512], kt == 0, kt == KT - 1,
                               [BWv, BaT[t]], [Bpv])
                        act(vg_[:, half * 512:(half + 1) * 512], pv, AF.Gelu, [Bpv], [Bvg_])
                    for half in range(2):
                        P.op("vector", lambda e, half=half: e.bn_stats(out=bst_[:, half * 6:(half + 1) * 6],
                                                                       in_=vg_[:, half * 512:(half + 1) * 512]), [Bvg_], [Bbst_])
                    P.op("vector", lambda e: e.bn_aggr(out=bst_[:, 12:14], in_=bst_[:, 0:12]), [Bbst_], [Bbst_])
                    act(bst_[:, 14:15], bst_[:, 13:14], AF.Ln, [Bbst_, Beps], [Bbst_], bias=eps6[:, 1:2])
                    act(bst_[:, 15:16], bst_[:, 14:15], AF.Exp, [Bbst_], [Bbst_], scale=-0.5)
                    ts("vector", vn_, vg_, bst_[:, 12:13], bst_[:, 15:16], ALU.subtract, ALU.mult, [Bvg_, Bbst_], [Bvn_])

                def sgu2(ti, t):
                    k = ti % 2
                    vn_, Bvn_ = vn2[k], Bvn2[k]
                    lt = slice(ti * 128, (ti + 1) * 128)
                    psv, Bpsv = next_pw()
                    for g in range(8):
                        mm(psv[:, g * 128:(g + 1) * 128], vn_[:, g * 128:(g + 1) * 128], wsT[:, g, :], True, True,
                           [Bvn_, Bws], [Bpsv[g // 4]])
                    big, Bbig = v3(vg2[k], 8), Bvg2[k]
                    tt("vector", big, v3(psv, 8), bc(vec[:, VC["sgu_ln_w"]:VC["sgu_ln_w"] + 8], [128, 8, 128]), ALU.mult,
                       [Bpsv[0], Bpsv[1], Bvec], [Bbig])
                    tt("vector", big, big, Cst, ALU.add, [Bbig, BCst], [Bbig])
                    tt("vector", sT[:, :, lt], big, uT[:, :, lt], ALU.mult, [Bbig, BuT], [BsT[ti]])

                sgu1(0, tiles[0])
                P.pump(DA, 20)
                for ti, t in enumerate(tiles):
                    if ti + 1 < len(tiles):
                        sgu1(ti + 1, tiles[ti + 1])
                        P.pump(DA, 20)
                    sgu2(ti, t)
                    P.pump(DA, 20)
                while DA:
                    P.pump(DA, 1000)
                P.defer = DA
                for m in range(KT):
                    ms = slice(m * 128, (m + 1) * 128)
                    py, Bpy = pw[2][:, 0:512], Bpw[2][0]
                    pg, Bpg = pw[2][:, 512:1024], Bpw[2][1]
                    for d_ in range(KT):
                        mm(py[:, 0:SA], Wp[:, d_, ms], sT[:, d_, :], d_ == 0, d_ == KT - 1, [BWp] + BsT, [Bpy])
                    for kt in range(KT):
                        mm(pg[:, 0:SA], Wg[:, kt, ms], aT[:, kt, stok], kt == 0, kt == KT - 1, [BWg] + RaT, [Bpg])
                    act(sgt[m % 2], pg[:, 0:SA], AF.Sigmoid, [Bpg], [Bsgt[m % 2]])
                    tt("vector", mtmp[m % 2], py[:, 0:SA], sgt[m % 2], ALU.mult, [Bpy, Bsgt[m % 2]], [Bmtmp[m % 2]])
                    tt("vector", MB[:, m, stok], MB[:, m, stok], mtmp[m % 2], ALU.add, [Bmtmp[m % 2]] + [BMB[t] for t in tiles],
                       [BMB[t] for t in tiles])
                P.defer = None
            while DA:
                P.pump(DA, 1000)
            nrot[0] = 3
        P.barrier()
        if "mix" in dbg_d:
            P.dma("sync", lambda e: e.dma_start(out=v3(dbg_d["mix"], KT), in_=MB), BMB, [Bdbg])

        ACC_OFF = AT_OFF
        FT_OFF = ACC_OFF + NT * 4096
        acc = view(ACC_OFF, [128, NT, D], F32)
        Bacc = [Buf("acc%d" % i) for i in range(NT)]
        fT = view(FT_OFF, [128, KT, T], BF16)
        BfT = [Buf("fT%d" % i) for i in range(NT)]
        C1_OFF = FT_OFF + 32768
        NBLK = 4
        w1v = w1_d.rearrange("(kt p) c -> p kt c", p=128)
        w2v = w2_d.rearrange("(j p) c -> p j c", p=128)
        slots = []
        for off in (C1_OFF, MB_OFF):
            slots.append((view(off, [128, KT, 1024], BF16), view(off + 16384, [128, 8, 1024], BF16)))
        BW1 = [Buf("W1_0"), Buf("W1_1")]
        BW2 = [Buf("W2_0"), Buf("W2_1")]

        def load_blk(blk):
            W1b, W2b = slots[blk % 2]
            load_w(W1b, w1v[:, :, blk * 1024:(blk + 1) * 1024], BW1[blk % 2])
            load_w(W2b, w2v[:, blk * 8:(blk + 1) * 8, :], BW2[blk % 2])

        load_blk(0)
        for i in range(NT + 1):
            if i < NT:
                tok = slice(i * 128, (i + 1) * 128)
                xb_, Bxb = xbuf[i % 2], Bx[i % 2]
                ld("sync", xb_, x_d[i * 128:(i + 1) * 128, :], [Bxb])
                ph, Bph = next_pw()
                for half in range(2):
                    for m in range(KT):
                        mm(ph[:, half * 512:(half + 1) * 512], MB[:, m, tok], Wo[:, m, half * 512:(half + 1) * 512], m == 0,
                           m == KT - 1, [BWo, BMB[i]], [Bph[half]])
                for half in range(2):
                    hs = slice(half * 512, (half + 1) * 512)
                    tt("vector", acc[:, i, hs], ph[:, hs], xb_[:, hs], ALU.add, [Bph[half], Bxb], [Bacc[i]])
                norm_a(acc[:, i, :], Bacc[i], i)
            if i > 0:
                norm_b(VC["g_ffn"], fT, i - 1, BfT[i - 1])
        P.barrier()

        HID_OFF = C1_OFF + 32768
        hid = [view(HID_OFF + i * 8 * SA * 2, [128, 8, SA], BF16) for i in range(2)]
        Bhid = [Buf("hid0"), Buf("hid1")]
        hr = [view(HID_OFF + 2 * 8 * SA * 2 + i * SA * 2, [128, SA], BF16) for i in range(2)]
        Bhr = [Buf("hr0"), Buf("hr1")]
        assert HID_OFF + 2 * 8 * SA * 2 + 2 * SA * 2 <= TOP
        def store_out(t, src, Bsrc):
            P.dma("sync", lambda e: e.dma_start(out=out_d[t * 128:(t + 1) * 128, :], in_=src), [Bsrc], [Bout])

        load_blk(1)
        hc = 0
        for blk in range(NBLK):
            W1b, W2b = slots[blk % 2]
            for s_ in range(NSA):
                stok = slice(s_ * SA, (s_ + 1) * SA)
                tiles = list(range(s_ * TPS, (s_ + 1) * TPS))
                hb_, Bhb = hid[hc % 2], Bhid[hc % 2]
                hc += 1
                for j in range(8):
                    phd, Bphd = next_half()
                    for kt in range(KT):
                        mm(phd[:, 0:SA], W1b[:, kt, j * 128:(j + 1) * 128], fT[:, kt, stok], kt == 0, kt == KT - 1,
                           [BW1[blk % 2]] + [BfT[t] for t in tiles], [Bphd])
                    act(hr[j % 2], phd[:, 0:SA], AF.Relu, [Bphd], [Bhr[j % 2]])
                    tt("vector", hb_[:, j, :], hr[j % 2], hr[j % 2], ALU.mult, [Bhr[j % 2]], [Bhb])
                for ti, t in enumerate(tiles):
                    lt = slice(ti * 128, (ti + 1) * 128)
                    for half in range(2):
                        hs = slice(half * 512, (half + 1) * 512)
                        po, Bpo = next_half()
                        for j in range(8):
                            mm(po, hb_[:, j, lt], W2b[:, j, hs], j == 0, j == 7, [Bhb, BW2[blk % 2]], [Bpo])
                        tt("vector", acc[:, t, hs], acc[:, t, hs], po, ALU.add, [Bpo, Bacc[t]], [Bacc[t]])
                    if blk == NBLK - 1:
                        rstd, Bs = rms_rstd(acc[:, t, :], Bacc[t])
                        ot, Bot = xbuf[t % 2], Bx[t % 2]
                        stt(ot, acc[:, t, :], rstd, rowb[:, 0:1024], ALU.mult, ALU.mult, [Bacc[t], Bs, Brow], [Bot])
                        store_out(t, ot, Bot)
            if blk + 2 < NBLK:
                load_blk(blk + 2)
        P.final_wait("sync", [Bout, Bdbg])
        block = es.enter_context(nc.Block())
        P.emit(block)
    return nc


_NC_CACHE = {}


def _pack_vec(inp):
    v = np.zeros((128, NV), np.float32)

    def put(name, arr):
        a = np.asarray(arr, np.float32).reshape(-1)
        n = a.shape[0]
        full = n // 128
        if full:
            v[:, VC[name]:VC[name] + full] = a[:full * 128].reshape(full, 128).T
        rem = n - full * 128
        if rem:
            v[:rem, VC[name] + full] = a[full * 128:]

    put("g_mix", inp["g_mix"][0])
    put("sb0", inp["shift_b"][0, 0])
    put("sb1", inp["shift_b"][0, 1])
    for n in ["w0", "a0", "k_k", "k_a", "ln_x_w", "ln_x_b", "g_ffn", "sgu_ln_w", "sgu_ln_b", "r_k"]:
        put(n, inp[n][0])
    return v


def _common_inputs(inp):
    f32 = lambda a: np.ascontiguousarray(np.asarray(a, np.float32))
    rowB = np.zeros((128, 2048), np.float32)
    rowB[:, 0:1024] = np.asarray(inp["g_final"], np.float32)[None, :]
    rowB[:, 1024:2048] = np.asarray(inp["sgu_b"][0], np.float32).reshape(1, 1024)
    swT = np.ascontiguousarray(np.transpose(np.asarray(inp["sgu_w"][0], np.float32), (2, 0, 1)).reshape(128, 1024))
    return {
        "w_in": f32(inp["w_in"][0]), "w_proj_a": f32(inp["w_proj_a"][0]), "w_proj_b": f32(inp["w_proj_b"][0]),
        "w_out": f32(inp["w_out"][0]), "w_ffn1": f32(inp["w_ffn1"][0]), "w_ffn2": f32(inp["w_ffn2"][0]),
        "w_lora_w": f32(inp["w_lora_w"][0]), "a_lora_w": f32(inp["a_lora_w"][0]), "g_lora_w": f32(inp["g_lora_w"][0]),
        "sgu_wT": swT, "vecT": _pack_vec(inp), "rowB": rowB,
    }


def kernel(**inputs):
    x = np.asarray(inputs["x"], np.float32)
    B, T, _ = x.shape
    key = (T,)
    if key not in _NC_CACHE:
        _NC_CACHE[key] = build_nc(T)
    nc = _NC_CACHE[key]
    common = _common_inputs(inputs)
    in_maps = [dict(common, x=np.ascontiguousarray(x[b])) for b in range(B)]
    res = run_bass_kernel_spmd(nc, in_maps, core_ids=list(range(B)))
    return np.stack([np.asarray(r["out"], np.float32) for r in res.results], axis=0)
```

```python
import math
import numpy as np
import concourse.bass as bass
import concourse.mybir as mybir
from concourse.bass_utils import run_bass_kernel_spmd
from contextlib import ExitStack
from collections import deque

F32 = mybir.dt.float32
BF16 = mybir.dt.bfloat16
AF = mybir.ActivationFunctionType
ALU = mybir.AluOpType
AX = mybir.AxisListType

ENGS = ["tensor", "vector", "scalar", "gpsimd", "sync"]

D = 1024
KT = 8
DFF = 4096
CB = 3360
PTOT = 7456
CDEC = math.exp(-0.5)

VC = {}
_o = 0
for _n, _w in [("g_mix", 8), ("sb0", 27), ("sb1", 27), ("w0", 8), ("a0", 8), ("k_k", 8), ("k_a", 8),
               ("ln_x_w", 8), ("ln_x_b", 8), ("g_ffn", 8), ("sgu_ln_w", 8), ("sgu_ln_b", 8), ("r_k", 8)]:
    VC[_n] = _o
    _o += _w
NV = _o


class Buf:
    __slots__ = ("w", "r", "name")

    def __init__(self, name=""):
        self.w = None
        self.r = {}
        self.name = name


class Prog:
    def __init__(self, nc, stack, ndma=32):
        self.nc = nc
        self.sem = {e: stack.enter_context(nc.semaphore("s_" + e)) for e in ENGS}
        self.cnt = {e: 0 for e in ENGS}
        self.q = {e: [] for e in ENGS}
        self.seen = {e: {} for e in ENGS}
        self.evep = {}
        self.clk = {}
        self.locks = {}
        self.dsem = [stack.enter_context(nc.semaphore("d%d" % i)) for i in range(ndma)]
        self.dcnt = [0] * ndma
        self.dpool = {"gpsimd": list(range(0, ndma // 2)), "sync": list(range(ndma // 2, ndma))}
        self.dnext = {"gpsimd": 0, "sync": 0}

    DROP_SAME_WAR = True

    def _waits(self, eng, reads, writes):
        cand = {}
        seen = self.seen[eng]

        def need(ev):
            if ev is None:
                return
            key, sem, val = ev
            if eng == "tensor" and key == "tensor":
                return
            if seen.get(key, 0) >= val:
                return
            if key not in cand or cand[key][1] < val:
                cand[key] = (sem, val)

        for b in reads:
            need(b.w)
        same_ok = self.DROP_SAME_WAR and eng in ("vector", "scalar")
        for b in writes:
            if not (same_ok and b.w is not None and b.w[0] == eng):
                need(b.w)
            for ev in b.r.values():
                if same_ok and ev[0] == eng:
                    continue
                need(ev)
        out = []
        for key, (sem, val) in sorted(cand.items(), key=lambda kv: -kv[1][1]):
            if seen.get(key, 0) >= val:
                continue
            out.append((sem, val))
            seen[key] = val
            for k2, v2 in self.clk.get((key, val), {}).items():
                if seen.get(k2, 0) < v2:
                    seen[k2] = v2
        return out

    skip = False
    caps = None
    defer = None
    epoch = 0

    def mark(self, tag):
        if self.defer is not None:
            self.defer.append(("mark", None, tag, (), (), None))

    def _hop(self, item):
        kind, eng, fn, reads, writes, meta = item
        evs = [b.w for b in reads]
        for b in writes:
            evs += [b.w] + list(b.r.values())
        for ev in evs:
            if ev is None or ev[0] == eng:
                continue
            age = self.epoch - self.evep.get((ev[0], ev[2]), -100)
            if age < (3 if ev[0] == "gpsimd" else 1):
                return True
        return False

    def pump(self, queues, maxops=8):
        self.epoch += 1
        if isinstance(queues, deque):
            queues = [queues]
        caps = self.caps
        for D in queues:
            n = 0
            cnt = {}
            while D and n < maxops:
                if caps and D[0][0] != "mark" and cnt.get(D[0][1], 0) >= caps.get(D[0][1], 99):
                    break
                if D[0][0] == "mark":
                    me = id(D)
                    if D[0][2] == "open":
                        if any(v > 0 for k, v in self.locks.items() if k != me):
                            break
                        self.locks[me] = self.locks.get(me, 0) + 1
                    else:
                        self.locks[me] -= 1
                    D.popleft()
                    continue
                grp = [D[0]]
                meta = D[0][5]
                if meta is not None and not meta[1]:
                    for it in list(D)[1:]:
                        if it[0] == "mark":
                            continue
                        grp.append(it)
                        if it[5] is not None and it[5][1]:
                            break
                if any(self._hop(it) for it in grp):
                    break
                for it in grp:
                    while D[0][0] == "mark":
                        D.rotate(-1)
                    D.popleft()
                    kind, eng, fn, reads, writes, meta = it
                    (self.op if kind == "op" else self.dma)(eng, fn, reads, writes)
                    n += 1
                    cnt[eng] = cnt.get(eng, 0) + 1

    def op(self, eng, fn, reads=(), writes=(), meta=None):
        if self.skip:
            return
        if self.defer is not None:
            self.defer.append(("op", eng, fn, tuple(reads), tuple(writes), meta))
            return
        waits = self._waits(eng, reads, writes)
        self.cnt[eng] += 1
        ev = (eng, self.sem[eng], self.cnt[eng])
        self.evep[(eng, self.cnt[eng])] = self.epoch
        ck = dict(self.seen[eng])
        ck[eng] = self.cnt[eng] - 1
        self.clk[(eng, self.cnt[eng])] = ck
        for b in reads:
            b.r[eng] = ev
        for b in writes:
            b.w = ev
            b.r = {}
        self.q[eng].append((waits, fn, self.sem[eng], 1, meta != "noattach"))

    def dma(self, eng, fn, reads=(), writes=()):
        if self.skip:
            return
        if self.defer is not None:
            self.defer.append(("dma", eng, fn, tuple(reads), tuple(writes), None))
            return
        waits = self._waits(eng, reads, writes)
        pool = self.dpool[eng]
        i = pool[self.dnext[eng] % len(pool)]
        self.dnext[eng] += 1
        prev = 16 * self.dcnt[i]
        if prev > 0 and self.seen[eng].get("d%d" % i, 0) < prev:
            self.seen[eng]["d%d" % i] = prev
            waits.append((self.dsem[i], prev))
        self.dcnt[i] += 1
        ev = ("d%d" % i, self.dsem[i], 16 * self.dcnt[i])
        self.evep[(ev[0], ev[2])] = self.epoch
        self.clk[(ev[0], ev[2])] = dict(self.seen[eng])
        for b in reads:
            b.r[ev[0]] = ev
        for b in writes:
            b.w = ev
            b.r = {}
        self.q[eng].append((waits, fn, self.dsem[i], 16, False))

    def barrier(self):
        evs = [(e, self.sem[e], self.cnt[e]) for e in ENGS if self.cnt[e] > 0]
        evs += [("d%d" % i, self.dsem[i], 16 * self.dcnt[i]) for i in range(len(self.dsem)) if self.dcnt[i] > 0]
        for e in ENGS:
            seen = self.seen[e]
            waits = []
            for key, sem, val in evs:
                if key == e and e == "tensor":
                    continue
                if seen.get(key, 0) < val:
                    seen[key] = val
                    waits.append((sem, val))
            if waits:
                self.q[e].append((waits, None, None, 0, False))

    def final_wait(self, eng, bufs):
        waits = self._waits(eng, bufs, ())
        self.q[eng].append((waits, None, None, 0, False))

    def emit(self, block):
        eng_of = {self.sem[e]: e for e in ENGS}
        waited = {e: set() for e in ENGS}
        for e in ENGS:
            for waits, fn, sem, inc, att in self.q[e]:
                for (s_, v) in waits:
                    if s_ in eng_of:
                        waited[eng_of[s_]].add(v)
        rank = {e: {v: i + 1 for i, v in enumerate(sorted(waited[e]))} for e in ENGS}
        for e in ENGS:
            items = self.q[e]

            def f(eng, items=items, e=e):
                idx = 0
                for waits, fn, sem, inc, att in items:
                    tw = [(s_, rank[eng_of[s_]][v] if s_ in eng_of else v) for (s_, v) in waits]
                    ride = None
                    if att and tw and fn is not None and e in ("vector", "scalar", "gpsimd", "tensor") and sem in eng_of:
                        ride = tw.pop()
                    for (s_, v) in tw:
                        eng.wait_ge(s_, v)
                    if fn is not None:
                        ins = fn(eng)
                        if ride is not None:
                            ins._wait_ge(ride[0], ride[1])
                        if sem in eng_of:
                            idx += 1
                            if idx in waited[e]:
                                ins.then_inc(sem, inc)
                        else:
                            ins.then_inc(sem, inc)

            getattr(block, e)(f)


def build_nc(T, do_a=True, do_b=True, dbg=None, bstop=99):
    NT = T // 128
    SA = min(512, T)
    NSA = T // SA
    TPS = SA // 128
    nc = bass.Bass("TRN2", target_bir_lowering=False)
    dram = lambda n, s, k="ExternalInput": nc.dram_tensor(n, s, F32, kind=k).ap()
    x_d = dram("x", [T, D])
    win_d = dram("w_in", [D, PTOT])
    wpa_d = dram("w_proj_a", [D, D])
    wpb_d = dram("w_proj_b", [D, D])
    wout_d = dram("w_out", [D, D])
    w1_d = dram("w_ffn1", [D, DFF])
    w2_d = dram("w_ffn2", [DFF, D])
    lw_d = dram("w_lora_w", [64, D])
    la_d = dram("a_lora_w", [64, D])
    lg_d = dram("g_lora_w", [160, D])
    swT_d = dram("sgu_wT", [128, 8 * 128])
    vec_d = dram("vecT", [128, NV])
    row_d = dram("rowB", [128, 2048])
    out_d = dram("out", [T, D], "ExternalOutput")
    dbg_d = {}
    if dbg:
        for n, (s, dt) in dbg.items():
            dbg_d[n] = nc.dram_tensor("dbg_" + n, s, dt, kind="ExternalOutput").ap()

    win_v = win_d.rearrange("(kt p) c -> p kt c", p=128)

    with ExitStack() as es:
        P = Prog(nc, es)
        ARENA_BYTES = 212480
        arena = es.enter_context(nc.sbuf_tensor("arena", [128, ARENA_BYTES // 2], BF16))
        psm = lambda name, shape, dt: es.enter_context(nc.psum_tensor(name, shape, dt))

        def view(off, shape, dt):
            n = 1
            for s_ in shape[1:]:
                n *= s_
            if dt == BF16:
                assert off % 2 == 0
                v = arena[0:shape[0], off // 2: off // 2 + n]
            else:
                assert off % 4 == 0
                v = arena[0:shape[0], off // 2: off // 2 + 2 * n].bitcast(F32)
            if len(shape) == 3:
                v = v.rearrange("p (a b) -> p a b", a=shape[1])
            elif len(shape) == 4:
                v = v.rearrange("p (a b c) -> p a b c", a=shape[1], b=shape[2])
            return v

        class Bump:
            def __init__(self, base, limit):
                self.o = base
                self.limit = limit

            def get(self, shape, dt):
                n = 1
                for s_ in shape[1:]:
                    n *= s_
                nb = n * (2 if dt == BF16 else 4)
                nb = (nb + 63) // 64 * 64
                off = self.o
                self.o += nb
                assert self.o <= self.limit, ("arena overflow", self.o, self.limit)
                return view(off, shape, dt)

        def mm(out, lhsT, rhs, start, stop, r, w):
            P.op("tensor", lambda e: e.matmul(out, lhsT=lhsT, rhs=rhs, start=start, stop=stop), r, w, meta=(start, stop))

        def tr(out, in_, r, w):
            P.op("tensor", lambda e: e.transpose(out, in_, identb), r, w)

        def act(out, in_, func, r, w, bias=None, scale=None, accum=None):
            kw = {}
            if bias is not None:
                kw["bias"] = bias
            if scale is not None:
                kw["scale"] = scale
            if accum is not None:
                kw["accum_out"] = accum
            P.op("scalar", lambda e: e.activation(out=out, in_=in_, func=func, **kw), r, w,
                 meta="noattach" if accum is not None else None)

        def tt(eng, out, in0, in1, op, r, w):
            P.op(eng, lambda e: e.tensor_tensor(out=out, in0=in0, in1=in1, op=op), r, w)

        def ts(eng, out, in0, s1, s2, op0, op1, r, w):
            if s2 is None:
                P.op(eng, lambda e: e.tensor_scalar(out=out, in0=in0, scalar1=s1, scalar2=None, op0=op0), r, w)
            else:
                P.op(eng, lambda e: e.tensor_scalar(out=out, in0=in0, scalar1=s1, scalar2=s2, op0=op0, op1=op1), r, w)

        def stt(out, in0, scalar, in1, op0, op1, r, w):
            P.op("vector", lambda e: e.scalar_tensor_tensor(out=out, in0=in0, scalar=scalar, in1=in1, op0=op0, op1=op1), r, w)

        def cp(eng, out, in_, r, w):
            if eng == "scalar":
                P.op("scalar", lambda e: e.activation(out=out, in_=in_, func=AF.Copy), r, w)
            elif eng == "vector":
                P.op(eng, lambda e: e.tensor_scalar(out=out, in0=in_, scalar1=1.0, scalar2=None, op0=ALU.mult), r, w)
            else:
                P.op(eng, lambda e: e.tensor_copy(out=out, in_=in_), r, w)

        def ld(eng, out, in_, w, r=()):
            P.dma(eng, lambda e: e.dma_start(out=out, in_=in_), r, w)

        def mset(ap, val, w):
            P.op("gpsimd", lambda e: e.memset(ap, val), (), w)

        def bc(ap, shape):
            return ap.unsqueeze(2).to_broadcast(shape)

        def bcm(ap, n):
            return ap.unsqueeze(1).to_broadcast([128, n, 128])

        def v3(ap, a):
            return ap.rearrange("p (a b) -> p a b", a=a)

        def red(out, in_, r, w):
            P.op("vector", lambda e: e.tensor_reduce(out=out, in_=in_, axis=AX.X, op=ALU.add), r, w)

        pw = [psm("pw%d" % i, [128, 1024], F32) for i in range(3)]
        pt = [psm("pt%d" % i, [128, 1024], BF16) for i in range(2)]
        Bpw = [[Buf("pw%d_%d" % (i, h)) for h in range(2)] for i in range(3)]
        Bpt = [Buf("pt%d" % i) for i in range(2)]
        pwc = [0]
        left = [None]
        nrot = [3]

        def next_pw():
            left[0] = None
            i = pwc[0] % nrot[0]
            pwc[0] += 1
            return pw[i], Bpw[i]

        def next_half():
            if left[0] is not None:
                i = left[0]
                left[0] = None
                return pw[i][:, 512:1024], Bpw[i][1]
            i = pwc[0] % nrot[0]
            pwc[0] += 1
            left[0] = i
            return pw[i][:, 0:512], Bpw[i][0]

        ptc = [0]

        def next_pt():
            i = ptc[0] % 2
            ptc[0] += 1
            return pt[i], Bpt[i]

        G = Bump(0, 24576)
        vec = G.get([128, NV], F32)
        Bvec = Buf("vec")
        rowb = G.get([128, 2048], F32)
        Brow = Buf("rowb")
        identb = G.get([128, 128], BF16)
        Bid = Buf("ident")
        cf = G.get([128, 128], F32)
        Bcf = Buf("cf")
        onesb = G.get([128, 128], BF16)
        blk1 = G.get([128, 128], BF16)
        m_us = G.get([128, 128], BF16)
        m_ui = G.get([128, 128], BF16)
        m_ls = G.get([128, 128], BF16)
        Bmask = Buf("masks")
        omk = G.get([128, 8], F32)
        Bomk = Buf("omk")
        eps6 = G.get([128, 4], F32)
        Beps = Buf("eps")
        sml = G.get([128, 64], F32)
        Bsml = Buf("sml")
        xbuf = [G.get([128, D], F32) for i in range(2)]
        Bx = [Buf("xb%d" % i) for i in range(2)]
        xnb = G.get([128, D], BF16)
        Bxnb = Buf("xnb")
        junk = G.get([128, D], BF16)
        Bjunk = Buf("junk")
        MB_OFF = 24576
        AT_OFF = MB_OFF + 32768
        PH_OFF = AT_OFF + 32768
        TOP = ARENA_BYTES
        MB = view(MB_OFF, [128, KT, T], BF16)
        aT = view(AT_OFF, [128, KT, T], BF16)
        BMB = [Buf("mb%d" % i) for i in range(NT)]
        BaT = [Buf("aT%d" % i) for i in range(NT)]

        ld("sync", vec, vec_d, [Bvec])
        ld("sync", rowb, row_d, [Brow])

        def gen_mask(dst, cm, step, cmp_op):
            mset(cf, 1.0, [Bcf])
            P.op("gpsimd", lambda e: e.affine_select(out=cf, in_=cf, pattern=[[step, 128]], compare_op=cmp_op,
                                                     fill=0.0, base=0, channel_multiplier=cm), [Bcf], [Bcf])
            cp("vector", dst, cf, [Bcf], [Bmask])

        mset(cf, 0.0, [Bcf])
        P.op("gpsimd", lambda e: e.affine_select(out=cf, in_=cf, pattern=[[-1, 128]], compare_op=ALU.not_equal,
                                                 fill=1.0, base=0, channel_multiplier=1), [Bcf], [Bcf])
        cp("vector", identb, cf, [Bcf], [Bid])
        gen_mask(m_us, -1, 1, ALU.is_gt)
        gen_mask(m_ui, -1, 1, ALU.is_ge)
        gen_mask(m_ls, 1, -1, ALU.is_gt)
        mset(onesb, 1.0, [Bmask])
        mset(blk1, 0.0, [Bmask])
        mset(blk1[0:64, 0:64], 1.0, [Bmask])
        mset(blk1[64:128, 64:128], 1.0, [Bmask])
        ts("vector", omk, vec[:, VC["k_a"]:VC["k_a"] + 8], -1.0, 1.0, ALU.mult, ALU.add, [Bvec], [Bomk])
        mset(eps6[:, 0:1], 1e-6, [Beps])
        mset(eps6[:, 1:2], 1e-5, [Beps])
        mset(eps6[:, 2:3], 64e-5, [Beps])

        Bsml2 = [Buf("sml0"), Buf("sml1"), Buf("sml2")]
        xnb2, Bxnb2 = [xnb, junk], [Bxnb, Bjunk]
        nrm = [0]

        def rms_rstd(src, Bsrc, scratch=None, Bscr=None):
            k = nrm[0] % 2
            nrm[0] += 1
            if scratch is None:
                scratch, Bscr, k = junk, Bjunk, 2
            sm, Bs = sml[:, 4 * k:4 * k + 4], Bsml2[k]
            act(scratch, src, AF.Square, [Bsrc], [Bscr, Bs], accum=sm[:, 0:1])
            act(sm[:, 1:2], sm[:, 0:1], AF.Ln, [Bs, Beps], [Bs], bias=eps6[:, 0:1], scale=1.0 / D)
            act(sm[:, 2:3], sm[:, 1:2], AF.Exp, [Bs], [Bs], scale=-0.5)
            return sm[:, 2:3], Bs

        def norm_a(src, Bsrc, i):
            xn, Bxn = xnb2[i % 2], Bxnb2[i % 2]
            rstd, Bs = rms_rstd(src, Bsrc, xn, Bxn)
            ts("vector", xn, src, rstd, None, ALU.mult, None, [Bsrc, Bs], [Bxn])

        def norm_b(gcol, dstT, i, Bdst):
            xn, Bxn = xnb2[i % 2], Bxnb2[i % 2]
            ptt, Bp = next_pt()
            for kt in range(KT):
                tr(ptt[:, kt * 128:(kt + 1) * 128], xn[:, kt * 128:(kt + 1) * 128], [Bxn, Bid], [Bp])
            tt("vector", dstT[:, :, i * 128:(i + 1) * 128], v3(ptt, KT),
               bc(vec[:, gcol:gcol + 8], [128, KT, 128]), ALU.mult, [Bp, Bvec], [Bdst])

        def load_w(dst, src_view, Bw, split=2):
            n = dst.shape[1]
            step = n // split
            for s_ in range(split):
                ld("gpsimd", dst[:, s_ * step:(s_ + 1) * step, :], src_view[:, s_ * step:(s_ + 1) * step, :], [Bw])

        def dump(name, ap, Bsrc):
            if name in dbg_d:
                P.dma("sync", lambda e: e.dma_start(out=dbg_d[name], in_=ap), [Bsrc], [Bdbg])

        Bdbg = Buf("dbg")
        Bout = Buf("out")

        if do_b:
            preA = Bump(PH_OFF, TOP)
            Wrp_e = preA.get([128, KT, 1824], BF16)
            lwa_e = preA.get([128, D], BF16)
            lg1_e = preA.get([128, D], BF16)
            lg2_e = preA.get([32, D], BF16)
            BWr_e, BWl_e, Blw_e = Buf("Wrp"), Buf("Wlora"), Buf("lw")
            ld("gpsimd", Wrp_e[:, :, 1536:1824], win_v[:, :, 2048 + 3072:2048 + 3360], [BWl_e])
            for part in range(3):
                ld("gpsimd", Wrp_e[:, :, part * 512:(part + 1) * 512],
                   win_v[:, :, 2048 + part * 1024:2048 + part * 1024 + 512], [BWr_e])
            ld("gpsimd", lwa_e[0:64, :], lw_d, [Blw_e])
            ld("gpsimd", lwa_e[64:128, :], la_d, [Blw_e])
            ld("gpsimd", lg1_e, lg_d[0:128, :], [Blw_e])
            ld("gpsimd", lg2_e, lg_d[128:160, :], [Blw_e])

        for i in range(NT + 1):
            if i < NT:
                xb_, Bxb = xbuf[i % 2], Bx[i % 2]
                ld("sync", xb_, x_d[i * 128:(i + 1) * 128, :], [Bxb])
                norm_a(xb_, Bxb, i)
            if i > 0:
                norm_b(VC["g_mix"], aT, i - 1, BaT[i - 1])
        if "aT" in dbg_d:
            P.dma("sync", lambda e: e.dma_start(out=v3(dbg_d["aT"], KT), in_=aT), BaT, [Bdbg])

        if do_b:
            nrot[0] = 2
            A = Bump(PH_OFF, TOP)
            Wrp = A.get([128, KT, 1824], BF16)
            BWr = BWr_e
            BWl = BWl_e
            lwa = A.get([128, D], BF16)
            lg1 = A.get([128, D], BF16)
            lg2 = A.get([32, D], BF16)
            Blw = Blw_e
            Rk = A.get([128, KT, 128], BF16)
            BRk = Buf("Rk")
            rst = A.get([128, 4, 128], F32)
            Brst = Buf("rst")
            carry = A.get([128, 16], F32)
            Bcar = Buf("carry")
            q2 = [A.get([128, 12, 128], BF16) for _ in range(2)]
            Bq2 = [[Buf("q%d_%d" % (s_, i)) for i in range(3)] for s_ in range(2)]
            lq = A.get([128, 2, 128], F32)
            lq2 = A.get([32, 128], F32)
            Blq = Buf("lq")
            tmp4 = A.get([128, 4, 128], F32)
            Btmp4 = Buf("tmp4")
            t2 = A.get([128, 4, 128], F32)
            Bt2 = Buf("t2")
            lb = A.get([128, 128], BF16)
            sg1 = A.get([128, 128], BF16)
            sg2 = A.get([32, 128], BF16)
            Blb = Buf("lb")
            f4 = lambda: A.get([128, 4, 128], F32)
            b4 = lambda: A.get([128, 4, 128], BF16)
            sgw, cs, E1, E2, aa, kkf, sq, kpf = f4(), f4(), f4(), f4(), f4(), f4(), f4(), f4()
            Bsgw, Bcs, BE1, BE2, Baa, Bkkf, Bsq, Bkpf = [Buf(n) for n in "sgw cs E1 E2 aa kkf sq kpf".split()]
            kk2, rk_ = b4(), b4()
            Bkk2, Brk = Buf("kk2"), Buf("rk")
            SP = Bump(11200, 23488)
            at2, bt2, kt2 = [[b4(), b4()] for _ in range(3)]
            Bat2, Bbt2, Bkt2 = [[Buf(n + "0"), Buf(n + "1")] for n in "at bt kt".split()]
            rt2, gT2, bon2 = [[b4(), b4(), SP.get([128, 4, 128], BF16)] for _ in range(3)]
            Brt2, BgT2, Bbon2 = [[Buf(n + "0"), Buf(n + "1"), Buf(n + "2")] for n in "rt gT bon".split()]
            Wc2 = [A.get([128, 4], F32) for _ in range(3)]
            BWc2 = [Buf("Wc0"), Buf("Wc1"), Buf("Wc2")]
            m8 = lambda: A.get([128, 8, 128], BF16)
            Xs, Ls, Ys_ = [m8(), m8()], [m8(), m8()], [m8(), m8()]
            BXh = [[Buf("X%d%d" % (a, h)) for h in range(2)] for a in range(2)]
            BLh = [[Buf("L%d%d" % (a, h)) for h in range(2)] for a in range(2)]
            BYh_ = [[Buf("Y%d%d" % (a, h)) for h in range(2)] for a in range(2)]
            Ys1b = SP.get([128, 8, 128], BF16)
            BYh1b = [Buf("Y1b0"), Buf("Y1b1")]
            AkT = m8()
            BAk = Buf("AkT")
            ArbT2, ArkT2 = [m8(), SP.get([128, 8, 128], BF16)], [m8(), SP.get([128, 8, 128], BF16)]
            BArb2, BArk2 = [Buf("ArbT0"), Buf("ArbT1")], [Buf("ArkT0"), Buf("ArkT1")]
            btT2 = [A.get([128, 512], BF16), SP.get([128, 512], BF16)]
            ktT2 = [A.get([128, 512], BF16), SP.get([128, 512], BF16)]
            vT2 = [A.get([128, 512], BF16), SP.get([128, 512], BF16)]
            BbtT2, BktT2, BvT2 = [[Buf(n + "0"), Buf(n + "1")] for n in "btT ktT vT".split()]
            GT2 = [A.get([128, 4, 128], BF16), A.get([128, 4, 128], BF16)]
            BGT2 = [Buf("GT0"), Buf("GT1")]
            U = A.get([128, 8, 64], BF16)
            BU = Buf("U")
            ST = A.get([128, 4, 128], F32)
            STt = A.get([128, 4, 128], F32)
            STb = A.get([128, 4, 128], BF16)
            BST, BSTt, BSTb = Buf("ST"), Buf("STt"), Buf("STb")
            osq = A.get([128, 8, 64], F32)
            Bosq = Buf("osq")
            otmp = A.get([128, 8, 64], F32)
            Botmp = Buf("otmp")
            onb = A.get([128, 8, 64], BF16)
            Bonb = Buf("onb")
            gst = A.get([128, 64], F32)
            Bgst = Buf("gst")
            o2 = [A.get([128, 128], F32) for _ in range(2)]
            Bo2 = [Buf("o2a"), Buf("o2b")]
            ocp = A.get([128, 8, 64], F32)
            Bocp = Buf("ocp")
            flat = lambda ap: ap.rearrange("p a b -> p (a b)")
            pbc = [0]

            def next_pb():
                i = pbc[0] % 2
                pbc[0] += 1
                return pw[2][:, i * 512:(i + 1) * 512], Bpw[2][i]

            for f in range(KT):
                ts("vector", Rk[:, f, :], blk1, vec[:, VC["r_k"] + f:VC["r_k"] + f + 1], None, ALU.mult, None,
                   [Bmask, Bvec], [BRk])
            mset(rst, 1.0, [Brst])
            mset(rst[:, :, 0:1], 0.0, [Brst])

            def b12(hg, c):
                si_ = hg * NT + c
                sl, s3 = si_ % 2, si_ % 3
                q, Bq = q2[sl], Bq2[sl]
                at_, bt_, kt_, rt_, gT, bon, Wc = at2[sl], bt2[sl], kt2[sl], rt2[s3], gT2[s3], bon2[s3], Wc2[s3]
                Bat, Bbt, Bkt, Brt, BgT, Bbon, BWc = Bat2[sl], Bbt2[sl], Bkt2[sl], Brt2[s3], BgT2[s3], Bbon2[s3], BWc2[s3]
                tok = slice(c * 128, (c + 1) * 128)
                if c == 0:
                    if hg > 0:
                        for part in range(3):
                            ld("gpsimd", Wrp[:, :, part * 512:(part + 1) * 512],
                               win_v[:, :, 2048 + part * 1024 + hg * 512:2048 + part * 1024 + (hg + 1) * 512], [BWr])
                    mset(carry, 0.0, [Bcar])
                for bi in range(3):
                    j0 = bi * 8 + hg * 4
                    pq, Bpq = next_pb()
                    P.mark("open")
                    for fi in range(4):
                        for kt in range(KT):
                            mm(pq[:, fi * 128:(fi + 1) * 128], Wrp[:, kt, bi * 512 + fi * 128:bi * 512 + (fi + 1) * 128],
                               aT[:, kt, tok], kt == 0, kt == KT - 1, [BWr, BaT[c]], [Bpq])
                    pq3 = v3(pq, 4)
                    tt("vector", tmp4, pq3, bc(vec[:, VC["sb1"] + j0:VC["sb1"] + j0 + 4], [128, 4, 128]), ALU.mult,
                       [Bpq, Bvec], [Btmp4])
                    tt("vector", t2, pq3, bc(vec[:, VC["sb0"] + j0:VC["sb0"] + j0 + 4], [128, 4, 128]), ALU.mult,
                       [Bpq, Bvec], [Bt2])
                    P.mark("close")
                    qv = q[:, bi * 4:(bi + 1) * 4, :]
                    tt("vector", qv[:, :, 1:128], t2[:, :, 1:128], tmp4[:, :, 0:127], ALU.add, [Bt2, Btmp4], [Bq[bi]])
                    tt("vector", qv[:, :, 0:1], t2[:, :, 0:1], carry[:, bi * 4:(bi + 1) * 4].unsqueeze(2), ALU.add,
                       [Bt2, Bcar], [Bq[bi]])
                    ts("vector", carry[:, bi * 4:(bi + 1) * 4].unsqueeze(2), tmp4[:, :, 127:128], 1.0, None, ALU.mult, None,
                       [Btmp4], [Bcar])
                pq, Bpq = next_pb()
                P.mark("open")
                for li, (c0, c1, rows) in enumerate([(1536, 1664, 128), (1664, 1792, 128), (1792, 1824, 32)]):
                    for kt in range(KT):
                        mm(pq[0:rows, li * 128:(li + 1) * 128], Wrp[:, kt, c0:c1], aT[:, kt, tok], kt == 0, kt == KT - 1,
                           [BWl, BaT[c]], [Bpq])
                for li, rows, dst in [(0, 128, lq[:, 0, :]), (1, 128, lq[:, 1, :]), (2, 32, lq2)]:
                    jc = 24 + li
                    pql = pq[0:rows, li * 128:(li + 1) * 128]
                    s1 = vec[0:rows, VC["sb1"] + jc:VC["sb1"] + jc + 1]
                    s0 = vec[0:rows, VC["sb0"] + jc:VC["sb0"] + jc + 1]
                    tl = tmp4[0:rows, 0, :]
                    ts("vector", tl, pql, s1, None, ALU.mult, None, [Bpq, Bvec], [Btmp4])
                    stt(dst[:, 1:128], pql[:, 1:128], s0, tl[:, 0:127], ALU.mult, ALU.add, [Bpq, Bvec, Btmp4], [Blq])
                    stt(dst[:, 0:1], pql[:, 0:1], s0, carry[0:rows, 12 + li:13 + li], ALU.mult, ALU.add,
                        [Bpq, Bvec, Bcar], [Blq])
                    ts("vector", carry[0:rows, 12 + li:13 + li], tl[:, 127:128], 1.0, None, ALU.mult, None, [Btmp4], [Bcar])
                P.mark("close")
                act(lb[0:64, :], lq[0:64, 0, :], AF.Tanh, [Blq], [Blb])
                cp("vector", lb[64:128, :], lq[64:128, 0, :], [Blq], [Blb])
                act(sg1, lq[:, 1, :], AF.Sigmoid, [Blq], [Blb])
                act(sg2, lq2, AF.Sigmoid, [Blq], [Blb])
                pz, Bpz = next_pb()
                pa, Bpa = next_pb()
                P.mark("open")
                for fi in range(4):
                    f = hg * 4 + fi
                    fs = slice(f * 128, (f + 1) * 128)
                    cs_ = slice(fi * 128, (fi + 1) * 128)
                    mm(pz[:, cs_], lwa[0:64, fs], lb[0:64, :], True, True, [Blw, Blb], [Bpz])
                    mm(pa[:, cs_], lwa[64:128, fs], lb[64:128, :], True, True, [Blw, Blb], [Bpa])
                for fi in range(4):
                    f = hg * 4 + fi
                    cs_ = slice(fi * 128, (fi + 1) * 128)
                    act(sgw[:, fi, :], pz[:, cs_], AF.Sigmoid, [Bpz, Bvec], [Bsgw],
                        bias=vec[:, VC["w0"] + f:VC["w0"] + f + 1])
                    act(aa[:, fi, :], pa[:, cs_], AF.Sigmoid, [Bpa, Bvec], [Baa],
                        bias=vec[:, VC["a0"] + f:VC["a0"] + f + 1])
                P.mark("close")
                pg, Bpg = next_pb()
                P.mark("open")
                for fi in range(4):
                    f = hg * 4 + fi
                    fs = slice(f * 128, (f + 1) * 128)
                    cs_ = slice(fi * 128, (fi + 1) * 128)
                    mm(pg[:, cs_], lg1[:, fs], sg1, True, False, [Blw, Blb], [Bpg])
                    mm(pg[:, cs_], lg2[0:32, fs], sg2[0:32, :], False, True, [Blw, Blb], [Bpg])
                cp("scalar", gT, v3(pg, 4), [Bpg], [BgT])
                P.mark("close")
                P.op("vector", lambda e: e.tensor_tensor_scan(out=flat(cs), data0=flat(rst), data1=flat(sgw), initial=0.0,
                                                              op0=ALU.mult, op1=ALU.add), [Brst, Bsgw], [Bcs])
                tt("gpsimd", sgw, cs, sgw, ALU.subtract, [Bcs, Bsgw], [Bsgw])
                act(E1, cs, AF.Exp, [Bcs], [BE1], scale=-CDEC)
                act(E2, cs, AF.Exp, [Bcs], [BE2], scale=CDEC)
                act(sgw, sgw, AF.Exp, [Bsgw], [Bsgw], scale=-CDEC)
                ts("vector", Wc[:].unsqueeze(2), E1[:, :, 127:128], 1.0, None, ALU.mult, None, [BE1], [BWc])
                qr, qk, qv_ = q[:, 0:4, :], q[:, 4:8, :], q[:, 8:12, :]
                kcol = VC["k_k"] + hg * 4
                tt("vector", kkf, qk, bc(vec[:, kcol:kcol + 4], [128, 4, 128]), ALU.mult, [Bq[1], Bvec], [Bkkf])
                act(kk2, kkf, AF.Square, [Bkkf], [Bkk2])
                pss, Bpss = next_pb()
                P.mark("open")
                mm(pss, blk1, flat(kk2), True, True, [Bmask, Bkk2], [Bpss])
                ts("vector", sq, v3(pss, 4), 1e-24, None, ALU.max, None, [Bpss], [Bsq])
                P.mark("close")
                act(sq, sq, AF.Ln, [Bsq], [Bsq])
                act(sq, sq, AF.Exp, [Bsq], [Bsq], scale=-0.5)
                tt("vector", kkf, kkf, sq, ALU.mult, [Bkkf, Bsq], [Bkkf])
                stt(at_, kkf, -1.0, sgw, ALU.mult, ALU.mult, [Bkkf, Bsgw], [Bat])
                tt("gpsimd", sq, kkf, aa, ALU.mult, [Bkkf, Baa], [Bsq])
                tt("gpsimd", bt_, sq, E2, ALU.mult, [Bsq, BE2], [Bbt])
                acol = VC["k_a"] + hg * 4
                tt("vector", aa, aa, bc(vec[:, acol:acol + 4], [128, 4, 128]), ALU.mult, [Baa, Bvec], [Baa])
                tt("vector", aa, aa, bc(omk[:, hg * 4:hg * 4 + 4], [128, 4, 128]), ALU.add, [Baa, Bomk], [Baa])
                tt("vector", kpf, qk, aa, ALU.mult, [Bq[1], Baa], [Bkpf])
                tt("gpsimd", kt_, kpf, E2, ALU.mult, [Bkpf, BE2], [Bkt])
                tt("vector", rk_, qr, kpf, ALU.mult, [Bq[0], Bkpf], [Brk])
                tt("vector", rt_, qr, E1, ALU.mult, [Bq[0], BE1], [Brt])
                pbon, Bpbon = next_pb()
                P.mark("open")
                for fi in range(4):
                    f = hg * 4 + fi
                    mm(pbon[:, fi * 128:(fi + 1) * 128], Rk[:, f, :], rk_[:, fi, :], True, True, [BRk, Brk], [Bpbon])
                tt("vector", bon, v3(pbon, 4), qv_, ALU.mult, [Bpbon, Bq[2]], [Bbon])
                P.mark("close")

            def b3(hg, c, pump_):
                si_ = hg * NT + c
                sl, s3 = si_ % 2, si_ % 3
                q, Bq = q2[sl], Bq2[sl]
                at_, bt_, kt_, rt_, gT, bon, Wc = at2[sl], bt2[sl], kt2[sl], rt2[s3], gT2[s3], bon2[s3], Wc2[s3]
                Bat, Bbt, Bkt, Brt, BgT, Bbon, BWc = Bat2[sl], Bbt2[sl], Bkt2[sl], Brt2[s3], BgT2[s3], Bbon2[s3], BWc2[s3]
                GT, BGT = GT2[sl], BGT2[sl]
                btT, ktT, vT, BbtT, BktT, BvT = btT2[sl], ktT2[sl], vT2[sl], BbtT2[sl], BktT2[sl], BvT2[sl]
                ArbT, ArkT, BArb, BArk = ArbT2[sl], ArkT2[sl], BArb2[sl], BArk2[sl]
                Ys = [Ys_[0], Ys_[1] if sl == 0 else Ys1b]
                BYh = [BYh_[0], BYh_[1] if sl == 0 else BYh1b]
                qv_ = q[:, 8:12, :]
                tok = slice(c * 128, (c + 1) * 128)

                def pump():
                    if P.defer is None:
                        pump_()
                ptA, BptA = next_pt()
                ptB, BptB = next_pt()
                for fi in range(4):
                    cs_ = slice(fi * 128, (fi + 1) * 128)
                    cs2 = slice(512 + fi * 128, 512 + (fi + 1) * 128)
                    tr(ptA[:, cs_], at_[:, fi, :], [Bat, Bid], [BptA])
                    tr(ptA[:, cs2], bt_[:, fi, :], [Bbt, Bid], [BptA])
                    tr(ptB[:, cs_], kt_[:, fi, :], [Bkt, Bid], [BptB])
                    tr(ptB[:, cs2], qv_[:, fi, :], [Bq[2], Bid], [BptB])
                cp("vector", Ys[0][:, :, 0:64], v3(ptA[:, 0:512], 8), [BptA], BYh[0])
                cp("vector", btT, ptA[:, 512:1024], [BptA], [BbtT])
                cp("vector", ktT, ptB[:, 0:512], [BptB], [BktT])
                cp("vector", vT, ptB[:, 512:1024], [BptB], [BvT])
                pump()

                def amat(lh, Blh, rh, Brh, dst, Bdst, mask):
                    pk, Bpk = next_pw()
                    dst4 = dst.rearrange("q (f p) n -> q f p n", p=2)
                    for p in range(2):
                        ps_ = slice(p * 64, (p + 1) * 64)
                        for fi in range(4):
                            mm(pk[:, p * 512 + fi * 128:p * 512 + (fi + 1) * 128], lh[ps_, fi, :], rh[ps_, fi, :],
                               True, True, [Blh, Brh], [Bpk[p]])
                    pump()
                    for p in range(2):
                        tt("vector", dst4[:, :, p, :], v3(pk[:, p * 512:(p + 1) * 512], 4), bcm(mask, 4),
                           ALU.mult, [Bpk[p], Bmask], Bdst)
                    pump()

                amat(kt_, Bkt, at_, Bat, AkT, [BAk], m_us)
                amat(bt_, Bbt, at_, Bat, Xs[0], BXh[0], m_us)
                amat(at_, Bat, bt_, Bbt, Ls[0], BLh[0], m_ls)
                pwv, Bpwv = next_half()
                for hl in range(8):
                    mm(pwv[:, hl * 64:(hl + 1) * 64], AkT[:, hl, :], vT[:, hl * 64:(hl + 1) * 64], True, True,
                       [BAk, BvT], [Bpwv])
                cp("scalar", Ys[0][:, :, 64:128], v3(pwv, 8), [Bpwv], BYh[0])
                amat(bt_, Bbt, rt_, Brt, ArbT, [BArb], m_ui)
                amat(kt_, Bkt, rt_, Brt, ArkT, [BArk], m_ui)
                for lv in range(7):
                    a_, b_ = lv % 2, (lv + 1) % 2
                    for hb in range(2):
                        hs = slice(hb * 4, (hb + 1) * 4)
                        pY, BpY = next_half()
                        for j in range(4):
                            hl = hb * 4 + j
                            oy = pY[:, j * 128:(j + 1) * 128]
                            mm(oy, Xs[a_][:, hl, :], Ys[a_][:, hl, :], True, False, [BXh[a_][hb], BYh[a_][hb]], [BpY])
                            mm(oy, identb, Ys[a_][:, hl, :], False, True, [Bid, BYh[a_][hb]], [BpY])
                        pump()
                        if lv < 6:
                            pX, BpX = next_half()
                            for j in range(4):
                                hl = hb * 4 + j
                                mm(pX[:, j * 128:(j + 1) * 128], Ls[a_][:, hl, :], Xs[a_][:, hl, :], True, True,
                                   [BLh[a_][hb], BXh[a_][hb]], [BpX])
                            pump()
                        cp("scalar", Ys[b_][:, hs, :], v3(pY, 4), [BpY], [BYh[b_][hb]])
                        if lv < 5:
                            pL, BpL = next_half()
                            for j in range(4):
                                hl = hb * 4 + j
                                mm(pL[:, j * 128:(j + 1) * 128], Xs[a_][:, hl, :], Ls[a_][:, hl, :], True, True,
                                   [BXh[a_][hb], BLh[a_][hb]], [BpL])
                            pump()
                        if lv < 6:
                            cp("vector", Xs[b_][:, hs, :], v3(pX, 4), [BpX], [BXh[b_][hb]])
                        if lv < 5:
                            cp("scalar", Ls[b_][:, hs, :], v3(pL, 4), [BpL], [BLh[b_][hb]])
                        pump()
                Yf, BYf = Ys[1], BYh[1]
                ptG, BptG = next_pt()
                for hl in range(8):
                    fi, p = hl // 2, hl % 2
                    tr(ptG[p * 64:(p + 1) * 64, fi * 128:(fi + 1) * 128], Yf[:, hl, 0:64], BYf + [Bid], [BptG])
                cp("vector", GT, v3(ptG[:, 0:512], 4), [BptG], [BGT])
                pump()
                P.defer = DT
                if c == 0:
                    mset(ST, 0.0, [BST])
                    mset(STb, 0.0, [BSTb])
                P.mark("open")
                pU, BpU = pw[2], Bpw[2]
                U4 = U.rearrange("q (f p) n -> q f p n", p=2)
                for p in range(2):
                    ps_ = slice(p * 64, (p + 1) * 64)
                    for fi in range(4):
                        hl = 2 * fi + p
                        ou = pU[:, p * 512 + fi * 64:p * 512 + (fi + 1) * 64]
                        mm(ou, GT[ps_, fi, :], STb[ps_, fi, p * 64:(p + 1) * 64], True, False, [BGT, BSTb], [BpU[p]])
                        mm(ou, identb, Yf[:, hl, 64:128], False, True, [Bid] + BYf, [BpU[p]])
                for p in range(2):
                    cp("scalar", U4[:, :, p, :], v3(pU[:, p * 512:p * 512 + 256], 4), [BpU[p]], [BU])
                pO, BpO = pw[2], Bpw[2]
                for p in range(2):
                    ps_ = slice(p * 64, (p + 1) * 64)
                    for fi in range(4):
                        hl = 2 * fi + p
                        oo = pO[:, p * 512 + fi * 64:p * 512 + (fi + 1) * 64]
                        mm(oo, rt_[ps_, fi, :], STb[ps_, fi, p * 64:(p + 1) * 64], True, False, [Brt, BSTb], [BpO[p]])
                        mm(oo, ArbT[:, hl, :], U[:, hl, :], False, False, [BArb, BU], [BpO[p]])
                        mm(oo, ArkT[:, hl, :], vT[:, hl * 64:(hl + 1) * 64], False, True, [BArk, BvT], [BpO[p]])
                for fi in range(4):
                    cs_ = slice(fi * 128, (fi + 1) * 128)
                    os_ = pw[2][:, (fi // 2) * 512 + 256 + (fi % 2) * 128:(fi // 2) * 512 + 256 + (fi % 2 + 1) * 128]
                    mm(os_, btT[:, cs_], U[:, 2 * fi:2 * fi + 2, :].rearrange("p a b -> p (a b)"), True, False,
                       [BbtT, BU], [Bpw[2][fi // 2]])
                    mm(os_, ktT[:, cs_], vT[:, cs_], False, True, [BktT, BvT], [Bpw[2][fi // 2]])
                for b_ in range(2):
                    tt("vector", STt[:, 2 * b_:2 * b_ + 2, :], v3(pw[2][:, b_ * 512 + 256:b_ * 512 + 512], 2),
                       ST[:, 2 * b_:2 * b_ + 2, :], ALU.add, [Bpw[2][b_], BST], [BSTt])
                tt("vector", ST, STt, bc(Wc, [128, 4, 128]), ALU.mult, [BSTt, BWc], [BST])
                cp("scalar", STb, ST, [BST], [BSTb])
                cp("scalar", ocp[:, 0:4, :], v3(pO[:, 0:256], 4), [BpO[0]], [Bocp])
                cp("vector", ocp[:, 4:8, :], v3(pO[:, 512:768], 4), [BpO[1]], [Bocp])
                P.mark("close")
                red(gst[:, 0:8], ocp, [Bocp], [Bgst])
                act(osq, ocp, AF.Square, [Bocp], [Bosq])
                red(gst[:, 8:16], osq, [Bosq], [Bgst])
                ts("vector", gst[:, 16:24], gst[:, 0:8], 1.0 / 64, None, ALU.mult, None, [Bgst], [Bgst])
                tt("vector", gst[:, 24:32], gst[:, 16:24], gst[:, 16:24], ALU.mult, [Bgst], [Bgst])
                stt(gst[:, 32:40], gst[:, 8:16], 1.0 / 64, gst[:, 24:32], ALU.mult, ALU.subtract, [Bgst], [Bgst])
                act(gst[:, 40:48], gst[:, 32:40], AF.Ln, [Bgst, Beps], [Bgst], bias=eps6[:, 2:3])
                act(gst[:, 48:56], gst[:, 40:48], AF.Exp, [Bgst], [Bgst], scale=-0.5)
                tt("vector", otmp, ocp, bc(gst[:, 16:24], [128, 8, 64]), ALU.subtract, [Bocp, Bgst], [Botmp])
                onb4 = onb.rearrange("q (f p) n -> q p f n", p=2)
                tt("vector", onb4, otmp.rearrange("q (p f) n -> q p f n", p=2),
                   gst[:, 48:56].rearrange("q (p f) -> q p f", p=2).unsqueeze(3).to_broadcast([128, 2, 4, 64]),
                   ALU.mult, [Botmp, Bgst], [Bonb])
                ptO, BptO = next_pt()
                for fi in range(4):
                    tr(ptO[:, fi * 128:(fi + 1) * 128], onb[:, 2 * fi:2 * fi + 2, :].rearrange("p a b -> p (a b)"),
                       [Bonb, Bid], [BptO])
                for fi in range(4):
                    f = hg * 4 + fi
                    stt(o2[fi % 2], ptO[:, fi * 128:(fi + 1) * 128], vec[:, VC["ln_x_w"] + f:VC["ln_x_w"] + f + 1], bon[:, fi, :],
                        ALU.mult, ALU.add, [BptO, Bvec, Bbon], [Bo2[fi % 2]])
                    stt(MB[:, f, tok], o2[fi % 2], vec[:, VC["ln_x_b"] + f:VC["ln_x_b"] + f + 1], gT[:, fi, :],
                        ALU.add, ALU.mult, [Bo2[fi % 2], Bvec, BgT], [BMB[c]])
                P.defer = None

            DT = deque()
            steps = [(hg, c) for hg in range(2) for c in range(NT)]
            b12(*steps[0])
            P.barrier()

            def mkpump(DQ):
                def pump():
                    P.caps = {"tensor": 8, "vector": 2, "scalar": 2, "gpsimd": 1}
                    P.pump([DT, DQ], 10)
                    P.caps = None
                return pump

            for si, (hg, c) in enumerate(steps):
                DQ = deque()
                if si + 1 < len(steps):
                    P.defer = DQ
                    b12(*steps[si + 1])
                    P.defer = None
                b3(hg, c, mkpump(DQ))
                while DQ:
                    n0 = len(DQ)
                    P.pump(DQ, 1000)
                    if len(DQ) == n0:
                        P.pump(DT, 8)
            while DT:
                P.pump(DT, 1000)
            nrot[0] = 3
            if "obT" in dbg_d:
                P.dma("sync", lambda e: e.dma_start(out=v3(dbg_d["obT"], KT), in_=MB), BMB, [Bdbg])
        else:
            for c in range(NT):
                mset(MB[:, :, c * 128:(c + 1) * 128], 0.0, [BMB[c]])
        P.barrier()

        A1W = Bump(PH_OFF, PH_OFF + 65536)
        Wu_, Wv_, Wg1_, Wp1_ = [A1W.get([128, KT, D], BF16) for _ in range(4)]
        BWu, BWv, BWg1, BWp1 = Buf("Wu"), Buf("Wv"), Buf("Wg1"), Buf("Wp1")
        A = Bump(PH_OFF + 65536, PH_OFF + 65536 + 32768)
        Wg = A.get([128, KT, D], BF16)
        Wp = A.get([128, KT, D], BF16)
        BWg, BWp = Buf("Wg"), Buf("Wp")
        WO_OFF = TOP - 16384
        Wo = view(WO_OFF, [128, KT, D], BF16)
        BWo = Buf("Wo")
        SPg = Bump(11200, 23488)
        stage = SPg.get([128, KT, SA], BF16)
        Bstage = Buf("stage")
        sgt = [SPg.get([128, SA], BF16) for _ in range(2)]
        Bsgt = [Buf("sgt0"), Buf("sgt1")]
        BWp8 = [Buf("Wp2_%d" % m) for m in range(KT)]
        BWg8 = [Buf("Wg2_%d" % m) for m in range(KT)]
        wpb_v = wpb_d.rearrange("(kt p) c -> p kt c", p=128)
        for m in range(0, KT, 2):
            ms2 = slice(m * 128, (m + 2) * 128)
            ld("gpsimd", Wp[:, :, ms2], wpb_v[:, :, ms2], [BWp8[m], BWp8[m + 1]])
            ld("gpsimd", Wg[:, :, ms2], win_v[:, :, 2048 + CB + D + m * 128:2048 + CB + D + (m + 2) * 128], [BWg8[m], BWg8[m + 1]])
        if do_a:
            load_w(Wu_, win_v[:, :, 0:D], BWu)
            load_w(Wv_, win_v[:, :, D:2 * D], BWv)
            load_w(Wp1_, wpa_d.rearrange("(kt p) c -> p kt c", p=128), BWp1)
            load_w(Wg1_, win_v[:, :, 2048 + CB:2048 + CB + D], BWg1)
        load_w(Wo, wout_d.rearrange("(kt p) c -> p kt c", p=128), BWo)
        for s_ in range(NSA):
            stok = slice(s_ * SA, (s_ + 1) * SA)
            tiles = list(range(s_ * TPS, (s_ + 1) * TPS))
            for m in range(KT):
                ms = slice(m * 128, (m + 1) * 128)
                py, Bpy = next_half()
                pg, Bpg = next_half()
                for f in range(KT):
                    mm(py[:, 0:SA], Wp[:, f, ms], MB[:, f, stok], f == 0, f == KT - 1, [BWp8[m]] + [BMB[t] for t in tiles], [Bpy])
                for kt in range(KT):
                    mm(pg[:, 0:SA], Wg[:, kt, ms], aT[:, kt, stok], kt == 0, kt == KT - 1, [BWg8[m]] + [BaT[t] for t in tiles], [Bpg])
                act(sgt[m % 2], pg[:, 0:SA], AF.Sigmoid, [Bpg], [Bsgt[m % 2]])
                tt("vector", stage[:, m, :], py[:, 0:SA], sgt[m % 2], ALU.mult, [Bpy, Bsgt[m % 2]], [Bstage])
            cp("vector", MB[:, :, stok], stage, [Bstage], [BMB[t] for t in tiles])
        P.barrier()

        if do_a:
            A = Bump(PH_OFF + 65536, WO_OFF)
            Wu, Wv, Wg, Wp = Wu_, Wv_, Wg1_, Wp1_
            BWg, BWp = BWg1, BWp1
            wsf = A.get([128, 8, 128], F32)
            wsT = A.get([128, 8, 128], BF16)
            Bws = Buf("ws")
            Cst = A.get([128, 8, 128], F32)
            BCst = Buf("Cst")
            uT = A.get([128, KT, SA], BF16)
            BuT = Buf("uT")
            sT = A.get([128, KT, SA], BF16)
            BsT = [Buf("sT%d" % i) for i in range(TPS)]
            vg = A.get([128, D], F32)
            Bvg = Buf("vg")
            vn = A.get([128, D], BF16)
            Bvn = Buf("vn")
            bst = A.get([128, 16], F32)
            Bbst = Buf("bst")
            stmp = [A.get([128, 128], F32) for _ in range(2)]
            Bstmp = [Buf("stmp0"), Buf("stmp1")]
            sgt = [A.get([128, SA], BF16) for _ in range(2)]
            Bsgt = [Buf("sgt0"), Buf("sgt1")]
            mtmp = [A.get([128, SA], BF16) for _ in range(2)]
            Bmtmp = [Buf("mtmp0"), Buf("mtmp1")]
            SPa = Bump(11200, 23488)
            vg2, vn2, bst2 = [vg, SPa.get([128, D], F32)], [vn, A.get([128, D], BF16)], [bst, A.get([128, 16], F32)]
            Bvg2, Bvn2, Bbst2 = [Bvg, Buf("vg1")], [Bvn, Buf("vn1")], [Bbst, Buf("bst1")]
            sT2 = [sT, SPa.get([128, KT, SA], BF16)]
            BsT2 = [BsT, [Buf("sTb%d" % i) for i in range(TPS)]]
            DA = deque()
            nrot[0] = 2
            ld("sync", wsf.rearrange("p a b -> p (a b)"), swT_d, [Bws])
            tt("vector", wsT, wsf, bcm(m_ui, 8), ALU.mult, [Bws, Bmask], [Bws])
            for g in range(8):
                prs, Bprs = next_half()
                mm(prs[:, 0:128], onesb, wsT[:, g, :], True, True, [Bmask, Bws], [Bprs])
                stt(Cst[:, g, :], prs[:, 0:128], vec[:, VC["sgu_ln_b"] + g:VC["sgu_ln_b"] + g + 1],
                    rowb[:, 1024 + g * 128:1024 + (g + 1) * 128], ALU.mult, ALU.add, [Bprs, Bvec, Brow], [BCst])
            for s_ in range(NSA):
                stok = slice(s_ * SA, (s_ + 1) * SA)
                tiles = list(range(s_ * TPS, (s_ + 1) * TPS))
                RaT = [BaT[t] for t in tiles]
                sT, BsT = sT2[s_ % 2], BsT2[s_ % 2]
                for f in range(KT):
                    pu, Bpu = next_half()
                    for kt in range(KT):
                        mm(pu[:, 0:SA], Wu[:, kt, f * 128:(f + 1) * 128], aT[:, kt, stok], kt == 0, kt == KT - 1,
                           [BWu] + RaT, [Bpu])
                    act(uT[:, f, :], pu[:, 0:SA], AF.Gelu, [Bpu], [BuT])
                def sgu1(ti, t):
                    k = ti % 2
                    vg_, vn_, bst_, Bvg_, Bvn_, Bbst_ = vg2[k], vn2[k], bst2[k], Bvg2[k], Bvn2[k], Bbst2[k]
                    tok = slice(t * 128, (t + 1) * 128)
                    for half in range(2):
                        pv, Bpv = next_half()
                        for kt in range(KT):
                            mm(pv, aT[:, kt, tok], Wv[:, kt, half * 512:(half + 1) * 512], kt == 0, kt == KT - 1,
                               [BWv, BaT[t]], [Bpv])
                        act(vg_[:, half * 512:(half + 1) * 512], pv, AF.Gelu, [Bpv], [Bvg_])
                    for half in range(2):
                        P.op("vector", lambda e, half=half: e.bn_stats(out=bst_[:, half * 6:(half + 1) * 6],
                                                                       in_=vg_[:, half * 512:(half + 1) * 512]), [Bvg_], [Bbst_])
                    P.op("vector", lambda e: e.bn_aggr(out=bst_[:, 12:14], in_=bst_[:, 0:12]), [Bbst_], [Bbst_])
                    act(bst_[:, 14:15], bst_[:, 13:14], AF.Ln, [Bbst_, Beps], [Bbst_], bias=eps6[:, 1:2])
                    act(bst_[:, 15:16], bst_[:, 14:15], AF.Exp, [Bbst_], [Bbst_], scale=-0.5)
                    ts("vector", vn_, vg_, bst_[:, 12:13], bst_[:, 15:16], ALU.subtract, ALU.mult, [Bvg_, Bbst_], [Bvn_])

                def sgu2(ti, t):
                    k = ti % 2
                    vn_, Bvn_ = vn2[k], Bvn2[k]
                    lt = slice(ti * 128, (ti + 1) * 128)
                    psv, Bpsv = next_pw()
                    for g in range(8):
                        mm(psv[:, g * 128:(g + 1) * 128], vn_[:, g * 128:(g + 1) * 128], wsT[:, g, :], True, True,
                           [Bvn_, Bws], [Bpsv[g // 4]])
                    big, Bbig = v3(vg2[k], 8), Bvg2[k]
                    tt("vector", big, v3(psv, 8), bc(vec[:, VC["sgu_ln_w"]:VC["sgu_ln_w"] + 8], [128, 8, 128]), ALU.mult,
                       [Bpsv[0], Bpsv[1], Bvec], [Bbig])
                    tt("vector", big, big, Cst, ALU.add, [Bbig, BCst], [Bbig])
                    tt("vector", sT[:, :, lt], big, uT[:, :, lt], ALU.mult, [Bbig, BuT], [BsT[ti]])

                sgu1(0, tiles[0])
                P.pump(DA, 32)
                for ti, t in enumerate(tiles):
                    if ti + 1 < len(tiles):
                        sgu1(ti + 1, tiles[ti + 1])
                        P.pump(DA, 32)
                    sgu2(ti, t)
                    P.pump(DA, 32)
                while DA:
                    P.pump(DA, 1000)
                P.defer = DA
                for m in range(KT):
                    ms = slice(m * 128, (m + 1) * 128)
                    py, Bpy = pw[2][:, 0:512], Bpw[2][0]
                    pg, Bpg = pw[2][:, 512:1024], Bpw[2][1]
                    for d_ in range(KT):
                        mm(py[:, 0:SA], Wp[:, d_, ms], sT[:, d_, :], d_ == 0, d_ == KT - 1, [BWp] + BsT, [Bpy])
                    for kt in range(KT):
                        mm(pg[:, 0:SA], Wg[:, kt, ms], aT[:, kt, stok], kt == 0, kt == KT - 1, [BWg] + RaT, [Bpg])
                    act(sgt[m % 2], pg[:, 0:SA], AF.Sigmoid, [Bpg], [Bsgt[m % 2]])
                    tt("vector", mtmp[m % 2], py[:, 0:SA], sgt[m % 2], ALU.mult, [Bpy, Bsgt[m % 2]], [Bmtmp[m % 2]])
                    tt("vector", MB[:, m, stok], MB[:, m, stok], mtmp[m % 2], ALU.add, [Bmtmp[m % 2]] + [BMB[t] for t in tiles],
                       [BMB[t] for t in tiles])
                P.defer = None
            while DA:
                P.pump(DA, 1000)
            nrot[0] = 3
        P.barrier()
        if "mix" in dbg_d:
            P.dma("sync", lambda e: e.dma_start(out=v3(dbg_d["mix"], KT), in_=MB), BMB, [Bdbg])

        ACC_OFF = AT_OFF
        FT_OFF = ACC_OFF + NT * 4096
        acc = view(ACC_OFF, [128, NT, D], F32)
        Bacc = [Buf("acc%d" % i) for i in range(NT)]
        fT = view(FT_OFF, [128, KT, T], BF16)
        BfT = [Buf("fT%d" % i) for i in range(NT)]
        C1_OFF = FT_OFF + 32768
        NBLK = 4
        w1v = w1_d.rearrange("(kt p) c -> p kt c", p=128)
        w2v = w2_d.rearrange("(j p) c -> p j c", p=128)
        slots = []
        for off in (C1_OFF, MB_OFF):
            slots.append((view(off, [128, KT, 1024], BF16), view(off + 16384, [128, 8, 1024], BF16)))
        BW1 = [Buf("W1_0"), Buf("W1_1")]
        BW2 = [Buf("W2_0"), Buf("W2_1")]

        def load_blk(blk):
            W1b, W2b = slots[blk % 2]
            load_w(W1b, w1v[:, :, blk * 1024:(blk + 1) * 1024], BW1[blk % 2])
            load_w(W2b, w2v[:, blk * 8:(blk + 1) * 8, :], BW2[blk % 2])

        load_blk(0)
        for i in range(NT + 1):
            if i < NT:
                tok = slice(i * 128, (i + 1) * 128)
                xb_, Bxb = xbuf[i % 2], Bx[i % 2]
                ld("sync", xb_, x_d[i * 128:(i + 1) * 128, :], [Bxb])
                ph, Bph = next_pw()
                for half in range(2):
                    for m in range(KT):
                        mm(ph[:, half * 512:(half + 1) * 512], MB[:, m, tok], Wo[:, m, half * 512:(half + 1) * 512], m == 0,
                           m == KT - 1, [BWo, BMB[i]], [Bph[half]])
                for half in range(2):
                    hs = slice(half * 512, (half + 1) * 512)
                    tt("vector", acc[:, i, hs], ph[:, hs], xb_[:, hs], ALU.add, [Bph[half], Bxb], [Bacc[i]])
                norm_a(acc[:, i, :], Bacc[i], i)
            if i > 0:
                norm_b(VC["g_ffn"], fT, i - 1, BfT[i - 1])
        P.barrier()

        HID_OFF = C1_OFF + 32768
        hid = [view(HID_OFF + i * 8 * SA * 2, [128, 8, SA], BF16) for i in range(2)]
        Bhid = [Buf("hid0"), Buf("hid1")]
        hr = [view(HID_OFF + 2 * 8 * SA * 2 + i * SA * 2, [128, SA], BF16) for i in range(2)]
        Bhr = [Buf("hr0"), Buf("hr1")]
        assert HID_OFF + 2 * 8 * SA * 2 + 2 * SA * 2 <= TOP
        def store_out(t, src, Bsrc):
            P.dma("sync", lambda e: e.dma_start(out=out_d[t * 128:(t + 1) * 128, :], in_=src), [Bsrc], [Bout])

        load_blk(1)
        hc = 0
        for blk in range(NBLK):
            W1b, W2b = slots[blk % 2]
            for s_ in range(NSA):
                stok = slice(s_ * SA, (s_ + 1) * SA)
                tiles = list(range(s_ * TPS, (s_ + 1) * TPS))
                hb_, Bhb = hid[hc % 2], Bhid[hc % 2]
                hc += 1
                for j in range(8):
                    phd, Bphd = next_half()
                    for kt in range(KT):
                        mm(phd[:, 0:SA], W1b[:, kt, j * 128:(j + 1) * 128], fT[:, kt, stok], kt == 0, kt == KT - 1,
                           [BW1[blk % 2]] + [BfT[t] for t in tiles], [Bphd])
                    act(hr[j % 2], phd[:, 0:SA], AF.Relu, [Bphd], [Bhr[j % 2]])
                    tt("vector", hb_[:, j, :], hr[j % 2], hr[j % 2], ALU.mult, [Bhr[j % 2]], [Bhb])
                for ti, t in enumerate(tiles):
                    lt = slice(ti * 128, (ti + 1) * 128)
                    for half in range(2):
                        hs = slice(half * 512, (half + 1) * 512)
                        po, Bpo = next_half()
                        for j in range(8):
                            mm(po, hb_[:, j, lt], W2b[:, j, hs], j == 0, j == 7, [Bhb, BW2[blk % 2]], [Bpo])
                        tt("vector", acc[:, t, hs], acc[:, t, hs], po, ALU.add, [Bpo, Bacc[t]], [Bacc[t]])
                    if blk == NBLK - 1:
                        rstd, Bs = rms_rstd(acc[:, t, :], Bacc[t])
                        ot, Bot = xbuf[t % 2], Bx[t % 2]
                        stt(ot, acc[:, t, :], rstd, rowb[:, 0:1024], ALU.mult, ALU.mult, [Bacc[t], Bs, Brow], [Bot])
                        store_out(t, ot, Bot)
            if blk + 2 < NBLK:
                load_blk(blk + 2)
        P.final_wait("sync", [Bout, Bdbg])
        block = es.enter_context(nc.Block())
        P.emit(block)
    return nc


_NC_CACHE = {}


def _pack_vec(inp):
    v = np.zeros((128, NV), np.float32)

    def put(name, arr):
        a = np.asarray(arr, np.float32).reshape(-1)
        n = a.shape[0]
        full = n // 128
        if full:
            v[:, VC[name]:VC[name] + full] = a[:full * 128].reshape(full, 128).T
        rem = n - full * 128
        if rem:
            v[:rem, VC[name] + full] = a[full * 128:]

    put("g_mix", inp["g_mix"][0])
    put("sb0", inp["shift_b"][0, 0])
    put("sb1", inp["shift_b"][0, 1])
    for n in ["w0", "a0", "k_k", "k_a", "ln_x_w", "ln_x_b", "g_ffn", "sgu_ln_w", "sgu_ln_b", "r_k"]:
        put(n, inp[n][0])
    return v


def _common_inputs(inp):
    f32 = lambda a: np.ascontiguousarray(np.asarray(a, np.float32))
    rowB = np.zeros((128, 2048), np.float32)
    rowB[:, 0:1024] = np.asarray(inp["g_final"], np.float32)[None, :]
    rowB[:, 1024:2048] = np.asarray(inp["sgu_b"][0], np.float32).reshape(1, 1024)
    swT = np.ascontiguousarray(np.transpose(np.asarray(inp["sgu_w"][0], np.float32), (2, 0, 1)).reshape(128, 1024))
    return {
        "w_in": f32(inp["w_in"][0]), "w_proj_a": f32(inp["w_proj_a"][0]), "w_proj_b": f32(inp["w_proj_b"][0]),
        "w_out": f32(inp["w_out"][0]), "w_ffn1": f32(inp["w_ffn1"][0]), "w_ffn2": f32(inp["w_ffn2"][0]),
        "w_lora_w": f32(inp["w_lora_w"][0]), "a_lora_w": f32(inp["a_lora_w"][0]), "g_lora_w": f32(inp["g_lora_w"][0]),
        "sgu_wT": swT, "vecT": _pack_vec(inp), "rowB": rowB,
    }


def kernel(**inputs):
    x = np.asarray(inputs["x"], np.float32)
    B, T, _ = x.shape
    key = (T,)
    if key not in _NC_CACHE:
        _NC_CACHE[key] = build_nc(T)
    nc = _NC_CACHE[key]
    common = _common_inputs(inputs)
    in_maps = [dict(common, x=np.ascontiguousarray(x[b])) for b in range(B)]
    res = run_bass_kernel_spmd(nc, in_maps, core_ids=list(range(B)))
    return np.stack([np.asarray(r["out"], np.float32) for r in res.results], axis=0)
```
